# Optimizing a Trainium2 kernel written in Bass

```python
import numpy as np
import jax
import jax.numpy as jnp
from jax import lax

D_MODEL = 1024
BATCH = 8
SEQ = 4096
DEPTH = 1
DEC_BATCH = 32
DEC_SEQ = 1
PAST_LEN = 16384
PAGE_SIZE = 128

A_GROUPS = ((128, 1), (512, 4), (2048, 16))
A_N_GROUPS = 3
A_HEADS_PER_GROUP = 4
A_HEAD_DIM = 64
A_HEADS = A_N_GROUPS * A_HEADS_PER_GROUP
A_WIDTH = A_HEADS * A_HEAD_DIM
A_OUT = A_HEADS_PER_GROUP * A_HEAD_DIM
ROPE_THETA = 10000.0
Q_BLOCK = 128
GLA_HEADS = 4
GLA_HEAD_DK = 128
GLA_HEAD_DV = 256
GLA_DK = GLA_HEADS * GLA_HEAD_DK
GLA_DV = GLA_HEADS * GLA_HEAD_DV
GLA_GATE_RANK = 16
GLA_GATE_NORM = 16.0
GLA_CHUNK = 64
D_FF = 4 * D_MODEL
IN_WIDTH = 3 * A_WIDTH + 2 * GLA_DK + 2 * GLA_DV + GLA_GATE_RANK + 2 * D_MODEL
EPS = 1e-6

kernel_name = 'dilated_gla_hybrid_step'


def rmsnorm(x, g):
    xf = x.astype(jnp.float32)
    y = xf * lax.rsqrt(jnp.mean(xf * xf, axis=-1, keepdims=True) + EPS)
    return (y * g.astype(jnp.float32)).astype(x.dtype)


def rope(x, pos):
    half = A_HEAD_DIM // 2
    inv = ROPE_THETA ** (-jnp.arange(half, dtype=jnp.float32) / half)
    ang = pos.astype(jnp.float32)[:, None] * inv[None, :]
    cos = jnp.cos(ang)[None, :, None, :]
    sin = jnp.sin(ang)[None, :, None, :]
    xf = x.astype(jnp.float32)
    x1, x2 = xf[..., :half], xf[..., half:]
    return jnp.concatenate([x1 * cos - x2 * sin, x2 * cos + x1 * sin], axis=-1).astype(x.dtype)


def project(u, w_in, w_gk2, b_gk, pos):
    b, t, _ = u.shape
    sizes = (A_WIDTH, A_WIDTH, A_WIDTH, GLA_DK, GLA_DK, GLA_DV, GLA_DV, GLA_GATE_RANK, D_MODEL, D_MODEL)
    offs = [int(o) for o in np.cumsum(sizes)[:-1]]
    qa, ka, va, qb, kb, vb, rb, glr, ga, gb = jnp.split(u @ w_in, offs, axis=-1)
    qa = rope(qa.reshape(b, t, A_HEADS, A_HEAD_DIM), pos)
    ka = rope(ka.reshape(b, t, A_HEADS, A_HEAD_DIM), pos)
    va = va.reshape(b, t, A_HEADS, A_HEAD_DIM)
    f32 = jnp.float32
    qb = qb.reshape(b, t, GLA_HEADS, GLA_HEAD_DK).astype(f32) * (GLA_HEAD_DK ** -0.5)
    kb = kb.reshape(b, t, GLA_HEADS, GLA_HEAD_DK).astype(f32)
    vb = vb.reshape(b, t, GLA_HEADS, GLA_HEAD_DV).astype(f32)
    gk = jax.nn.log_sigmoid((glr @ w_gk2 + b_gk).astype(f32)) / GLA_GATE_NORM
    gk = gk.reshape(b, t, GLA_HEADS, GLA_HEAD_DK)
    return qa, ka, va, qb, kb, vb, gk, rb, ga, gb


def gathered_attn(q, k, v, idx, valid):
    safe = jnp.maximum(idx, 0)
    kk = jnp.take(k, safe, axis=1).astype(jnp.float32)
    vv = jnp.take(v, safe, axis=1).astype(jnp.float32)
    s = jnp.einsum('bqhd,bqkhd->bqhk', q.astype(jnp.float32), kk) * (A_HEAD_DIM ** -0.5)
    s = jnp.where(valid[None, :, None, :], s, -jnp.inf)
    m = jnp.max(s, axis=-1)
    p = jnp.exp(s - m[..., None])
    l = jnp.sum(p, axis=-1)
    num = jnp.einsum('bqhk,bqkhd->bqhd', p, vv)
    return m, l, num


def merge_dilation_groups(stats):
    m = jnp.stack([st[0] for st in stats])
    l = jnp.stack([st[1] for st in stats])
    num = jnp.stack([st[2] for st in stats])
    w = jnp.exp(m - jnp.max(m, axis=0, keepdims=True))
    o = jnp.sum(w[..., None] * num, axis=0) / jnp.sum(w * l, axis=0)[..., None]
    b, q = o.shape[:2]
    return o.reshape(b, q, A_OUT)


def group_slice(g):
    return slice(g * A_HEADS_PER_GROUP, (g + 1) * A_HEADS_PER_GROUP)


def dilated_prompt(qa, ka, va):
    b, s = qa.shape[:2]
    kg = [ka[:, :, group_slice(g)] for g in range(A_N_GROUPS)]
    vg = [va[:, :, group_slice(g)] for g in range(A_N_GROUPS)]

    def block(bi):
        t0 = bi * Q_BLOCK
        t = t0 + jnp.arange(Q_BLOCK)
        qblk = lax.dynamic_slice_in_dim(qa, t0, Q_BLOCK, axis=1)
        stats = []
        for g, (window, dil) in enumerate(A_GROUPS):
            idx = t[:, None] - dil * jnp.arange(window // dil + 1)[None, :]
            stats.append(gathered_attn(qblk[:, :, group_slice(g)], kg[g], vg[g], idx, idx >= 0))
        return merge_dilation_groups(stats)

    out = lax.map(block, jnp.arange(s // Q_BLOCK))
    return out.transpose(1, 0, 2, 3).reshape(b, s, A_OUT).astype(qa.dtype)


def prompt_buffers(ka, va):
    s = ka.shape[1]
    out = []
    for g, (window, _) in enumerate(A_GROUPS):
        keep = min(window, s)
        out += [ka[:, s - keep:, group_slice(g)], va[:, s - keep:, group_slice(g)]]
    return out


def dilated_sample(qa, ka, va, bufs):
    t = qa.shape[1]
    i = jnp.arange(t)
    stats, new = [], []
    for g, (window, dil) in enumerate(A_GROUPS):
        hs = group_slice(g)
        k_buf, v_buf = bufs[2 * g], bufs[2 * g + 1]
        kc = jnp.concatenate([k_buf, ka[:, :, hs].astype(k_buf.dtype)], axis=1)
        vc = jnp.concatenate([v_buf, va[:, :, hs].astype(v_buf.dtype)], axis=1)
        idx = k_buf.shape[1] + i[:, None] - dil * jnp.arange(window // dil + 1)[None, :]
        stats.append(gathered_attn(qa[:, :, hs], kc, vc, idx, idx >= 0))
        total = kc.shape[1]
        keep = min(window, total)
        new += [kc[:, total - keep:], vc[:, total - keep:]]
    return merge_dilation_groups(stats).astype(qa.dtype), new


def gla_chunked(q, k, v, g, s0):
    b, t, h, _ = q.shape
    dv = v.shape[-1]
    n = t // GLA_CHUNK

    def chunks(z):
        return z.reshape(b, n, GLA_CHUNK, h, z.shape[-1]).transpose(1, 0, 3, 2, 4)

    causal = jnp.tril(jnp.ones((GLA_CHUNK, GLA_CHUNK), dtype=bool))

    def step(state, inp):
        qc, kc, vc, gc = inp
        cum = jnp.cumsum(gc, axis=2)
        diff = cum[:, :, :, None, :] - cum[:, :, None, :, :]
        decay = jnp.exp(jnp.where(causal[None, None, :, :, None], diff, -jnp.inf))
        scores = jnp.einsum('bhtsd,bhsd->bhts', qc[:, :, :, None, :] * decay, kc)
        o = (jnp.einsum('bhts,bhsv->bhtv', scores, vc)
             + jnp.einsum('bhtd,bhdv->bhtv', qc * jnp.exp(cum), state))
        last = cum[:, :, -1:, :]
        state = (jnp.exp(last[:, :, 0, :])[..., None] * state
                 + jnp.einsum('bhsd,bhsv->bhdv', kc * jnp.exp(last - cum), vc))
        return state, o

    state, o = lax.scan(step, s0, (chunks(q), chunks(k), chunks(v), chunks(g)))
    return o.transpose(1, 0, 3, 2, 4).reshape(b, t, h, dv), state


def gla_recurrent(q, k, v, g, s0):
    def step(state, inp):
        qt, kt, vt, gt = inp
        state = jnp.exp(gt)[..., None] * state + kt[..., None] * vt[:, :, None, :]
        return state, jnp.einsum('bhd,bhdv->bhv', qt, state)

    def tmajor(z):
        return z.transpose(1, 0, 2, 3)

    state, o = lax.scan(step, s0, (tmajor(q), tmajor(k), tmajor(v), tmajor(g)))
    return tmajor(o), state


def merge_and_ffn(x, oa, ob, rb, ga, gb, g_gla, w_pa, w_pb, w_o, g_norm2, w_up, w_down):
    b, t, _ = x.shape
    ob = rmsnorm(ob, g_gla) * jax.nn.silu(rb.reshape(b, t, GLA_HEADS, GLA_HEAD_DV).astype(jnp.float32))
    ob = ob.reshape(b, t, GLA_DV).astype(x.dtype)
    mix = jax.nn.sigmoid(ga) * (oa @ w_pa) + jax.nn.sigmoid(gb) * (ob @ w_pb)
    h = x + mix @ w_o
    f = jnp.square(jax.nn.relu(rmsnorm(h, g_norm2) @ w_up))
    return h + f @ w_down


def setup_inputs(seed: int = 0) -> dict:
    key = jax.random.key(seed)
    ks = jax.random.split(key, 24)

    def nrm(k, shape, scale):
        return jax.random.normal(k, shape, jnp.float32) * scale

    buf = [min(w, PAST_LEN) for w, _ in A_GROUPS]
    hg, hd = A_HEADS_PER_GROUP, A_HEAD_DIM
    return {
        'x_prompt': nrm(ks[0], (BATCH, SEQ, D_MODEL), 1.0),
        'x_sample': nrm(ks[1], (DEC_BATCH, DEC_SEQ, D_MODEL), 1.0),
        'cache_a1_k': nrm(ks[2], (DEPTH, DEC_BATCH, buf[0], hg, hd), 1.0),
        'cache_a1_v': nrm(ks[3], (DEPTH, DEC_BATCH, buf[0], hg, hd), 1.0),
        'cache_a2_k': nrm(ks[4], (DEPTH, DEC_BATCH, buf[1], hg, hd), 1.0),
        'cache_a2_v': nrm(ks[5], (DEPTH, DEC_BATCH, buf[1], hg, hd), 1.0),
        'cache_a3_k': nrm(ks[6], (DEPTH, DEC_BATCH, buf[2], hg, hd), 1.0),
        'cache_a3_v': nrm(ks[7], (DEPTH, DEC_BATCH, buf[2], hg, hd), 1.0),
        'state_gla': nrm(ks[8], (DEPTH, DEC_BATCH, GLA_HEADS, GLA_HEAD_DK, GLA_HEAD_DV), 0.1),
        'g_norm1': 1.0 + nrm(ks[9], (DEPTH, D_MODEL), 0.01),
        'w_in': nrm(ks[10], (DEPTH, D_MODEL, IN_WIDTH), D_MODEL ** -0.5),
        'w_gk2': nrm(ks[11], (DEPTH, GLA_GATE_RANK, GLA_DK), GLA_GATE_RANK ** -0.5),
        'b_gk': nrm(ks[12], (DEPTH, GLA_DK), 0.1),
        'g_gla': 1.0 + nrm(ks[13], (DEPTH, GLA_HEAD_DV), 0.01),
        'w_pa': nrm(ks[14], (DEPTH, A_OUT, D_MODEL), A_OUT ** -0.5),
        'w_pb': nrm(ks[15], (DEPTH, GLA_DV, D_MODEL), GLA_DV ** -0.5),
        'w_o': nrm(ks[16], (DEPTH, D_MODEL, D_MODEL), D_MODEL ** -0.5),
        'g_norm2': 1.0 + nrm(ks[17], (DEPTH, D_MODEL), 0.01),
        'w_up': nrm(ks[18], (DEPTH, D_MODEL, D_FF), D_MODEL ** -0.5),
        'w_down': nrm(ks[19], (DEPTH, D_FF, D_MODEL), D_FF ** -0.5),
        'g_final': 1.0 + nrm(ks[20], (D_MODEL,), 0.01),
    }


def reference(x_prompt, x_sample, cache_a1_k, cache_a1_v, cache_a2_k, cache_a2_v, cache_a3_k, cache_a3_v,
              state_gla, g_norm1, w_in, w_gk2, b_gk, g_gla, w_pa, w_pb, w_o, g_norm2, w_up, w_down, g_final):
    pos_p = jnp.arange(x_prompt.shape[1], dtype=jnp.int32)
    pos_s = PAST_LEN + jnp.arange(x_sample.shape[1], dtype=jnp.int32)
    caches = (cache_a1_k, cache_a1_v, cache_a2_k, cache_a2_v, cache_a3_k, cache_a3_v)
    xp, xs = x_prompt, x_sample
    p_new = [[] for _ in range(7)]
    s_new = [[] for _ in range(7)]
    for l in range(DEPTH):
        qa, ka, va, qb, kb, vb, gk, rb, ga, gb = project(rmsnorm(xp, g_norm1[l]), w_in[l], w_gk2[l], b_gk[l], pos_p)
        oa = dilated_prompt(qa, ka, va)
        s0 = jnp.zeros((xp.shape[0], GLA_HEADS, GLA_HEAD_DK, GLA_HEAD_DV), jnp.float32)
        ob, s_fin = gla_chunked(qb, kb, vb, gk, s0)
        xp = merge_and_ffn(xp, oa, ob, rb, ga, gb, g_gla[l], w_pa[l], w_pb[l], w_o[l], g_norm2[l], w_up[l], w_down[l])
        for j, z in enumerate(prompt_buffers(ka, va) + [s_fin.astype(state_gla.dtype)]):
            p_new[j].append(z)
        qa, ka, va, qb, kb, vb, gk, rb, ga, gb = project(rmsnorm(xs, g_norm1[l]), w_in[l], w_gk2[l], b_gk[l], pos_s)
        oa, bufs = dilated_sample(qa, ka, va, [c[l] for c in caches])
        ob, s_upd = gla_recurrent(qb, kb, vb, gk, state_gla[l].astype(jnp.float32))
        xs = merge_and_ffn(xs, oa, ob, rb, ga, gb, g_gla[l], w_pa[l], w_pb[l], w_o[l], g_norm2[l], w_up[l], w_down[l])
        for j, z in enumerate(bufs + [s_upd.astype(state_gla.dtype)]):
            s_new[j].append(z)
    y_prompt = rmsnorm(xp, g_final)
    y_sample = rmsnorm(xs, g_final)
    p_a1_k, p_a1_v, p_a2_k, p_a2_v, p_a3_k, p_a3_v, p_gla = [jnp.stack(z) for z in p_new]
    s_a1_k, s_a1_v, s_a2_k, s_a2_v, s_a3_k, s_a3_v, s_gla = [jnp.stack(z) for z in s_new]
    return (y_prompt, y_sample, p_a1_k, p_a1_v, p_a2_k, p_a2_v, p_a3_k, p_a3_v, p_gla,
            s_a1_k, s_a1_v, s_a2_k, s_a2_v, s_a3_k, s_a3_v, s_gla)
```

```python
import numpy as np
from contextlib import ExitStack
import concourse.bass as bass
import concourse.mybir as mybir
from concourse.bass_utils import run_bass_kernel_spmd

F32 = mybir.dt.float32
BF16 = mybir.dt.bfloat16
AF = mybir.ActivationFunctionType
ALU = mybir.AluOpType
AX = mybir.AxisListType

D = 1024
SEQ = 4096
NT = 32
PAST = 16384
GROUPS = ((128, 1), (512, 4), (2048, 16))
RING = (2, 5, 17)
RINGV = (3, 6, 18)
OFF = dict(qa=0, ka=768, va=1536, qb=2304, kb=2816, vb=3328, rb=4352, glr=5376, ga=5392, gb=6416)
INW = 7440
EPS = 1e-6
NSLAB = 6
DBG = dict(nprompt=8, sample=True, stop=None, ncores=8, setup=7)


class Res:
    __slots__ = ("name", "w", "r", "sem", "cnt", "const", "psum")

    def __init__(self, name, const=False, psum=False):
        self.psum = psum
        self.name = name
        self.w = None
        self.r = {}
        self.sem = None
        self.cnt = 0
        self.const = const


PE_LOG = []
OPLOG = {}
LAST_K = None


class PEProxy:
    def __init__(self, pe, kb):
        self._pe = pe
        self._kb = kb

    def matmul(self, *a, **kw):
        PE_LOG.append(self._kb.tag)
        return self._pe.matmul(*a, **kw)

    def __getattr__(self, n):
        return getattr(self._pe, n)


class KB:
    def __init__(self, nc, es):
        self.nc = nc
        self.es = es
        self.engs = {"pe": PEProxy(nc.tensor, self), "act": nc.scalar, "dve": nc.vector, "pool": nc.gpsimd, "sp": nc.sync}
        self.tag = "setup"
        self.sems = {}
        self.cnt = {}
        self.seen = {e: {} for e in self.engs}
        self.epoch = 0
        self.nsem = 0
        self.store_keys = set()

    def _sem(self, key):
        if key not in self.sems:
            self.nsem += 1
            self.sems[key] = self.es.enter_context(self.nc.semaphore(f"s{self.nsem}"))
            self.cnt[key] = 0
        return self.sems[key]

    def _wait(self, eng, key, val):
        if self.seen[eng].get(key, 0) >= val:
            return
        self.engs[eng].wait_ge(self.sems[key], val)
        self.seen[eng][key] = val

    def _deps(self, eng, reads, writes):
        deps = {}

        def add(tok):
            if tok is None:
                return
            k, v = tok
            if deps.get(k, 0) < v:
                deps[k] = v

        for r in reads:
            add(r.w)
            if r.psum:
                for k, v in r.r.items():
                    if not (k[0] == "e" and k[1] == eng):
                        add((k, v))
        for w in writes:
            add(w.w)
            for k, v in w.r.items():
                add((k, v))
        for k, v in deps.items():
            if eng == "pe" and k[0] == "e" and k[1] == "pe":
                continue
            self._wait(eng, k, v)

    def _commit(self, tok, reads, writes):
        k, v = tok
        for w in writes:
            w.w = tok
            w.r = {}
        for r in reads:
            if r.const:
                continue
            if r.r.get(k, 0) < v:
                r.r[k] = v

    def op(self, eng, fn, reads=(), writes=()):
        self._deps(eng, reads, writes)
        ins = fn(self.engs[eng])
        key = ("e", eng, self.epoch)
        sem = self._sem(key)
        self.cnt[key] += 1
        ins.then_inc(sem, 1)
        tok = (key, self.cnt[key])
        if eng == "pe":
            self.seen[eng][key] = self.cnt[key]
        if DBG.get("oplog"):
            import sys
            OPLOG[tok] = (self.tag, sys._getframe(1).f_lineno)
        self._commit(tok, reads, writes)

    def dma(self, eng, out, in_, semres, reads=(), writes=(), store=False, wnodep=()):
        self._deps(eng, reads, writes)
        writes = list(writes) + list(wnodep)
        ins = self.engs[eng].dma_start(out=out, in_=in_)
        key = ("d", semres.name)
        sem = self._sem(key)
        self.cnt[key] += 16
        ins.then_inc(sem, 16)
        tok = (key, self.cnt[key])
        if store:
            self.store_keys.add(key)
        self._commit(tok, reads, writes)

    def barrier(self):
        last = {}
        for key, c in self.cnt.items():
            if key[0] == "e" and c > 0:
                e = key[1]
                if e not in last or key[2] > last[e][0][2]:
                    last[e] = (key, c)
        for eng in self.engs:
            for e2, (key, c) in last.items():
                if e2 != eng:
                    self._wait(eng, key, c)

    def finish(self):
        for key in sorted(self.store_keys):
            self._wait("sp", key, self.cnt[key])


class Buf:
    def __init__(self, k, name, shape, dt, nres=1, const=False, psum=False, es=None):
        es = k.es if es is None else es
        if psum:
            self.t = es.enter_context(k.nc.psum_tensor(name, shape, dt))
        else:
            self.t = es.enter_context(k.nc.sbuf_tensor(name, shape, dt))
        self.res = [Res(f"{name}_{i}", const, psum) for i in range(nres)]

    @property
    def r(self):
        return self.res[0]


class Rot:
    def __init__(self, items):
        self.items = items
        self.i = 0

    def next(self):
        x = self.items[self.i % len(self.items)]
        self.i += 1
        return x


def build():
    nc = bass.Bass("TRN2", target_bir_lowering=False)

    def din(name, shape, dt=F32):
        return nc.dram_tensor(name, list(shape), dt, kind="ExternalInput").ap()

    def dout(name, shape):
        return nc.dram_tensor(name, list(shape), F32, kind="ExternalOutput").ap()

    def dint(name, shape, dt):
        return nc.dram_tensor(name, list(shape), dt, kind="Internal").ap()

    xp = din("xp", [SEQ, D])
    xs = din("xs", [4, D])
    cache = [(din(f"c{g}k", [4, GROUPS[g][0], 256]), din(f"c{g}v", [4, GROUPS[g][0], 256])) for g in range(3)]
    sg = din("sg", [4, 4, 128, 256])
    g1_d = din("g1", [128, 8])
    g2_d = din("g2", [128, 8])
    gf_d = din("gf", [128, D])
    gg_d = din("gg", [128, 256])
    w_in = din("w_in", [D, INW])
    wgk_d = din("wgk", [17, 512])
    w_pa = din("w_pa", [256, D])
    w_pb = din("w_pb", [D, D])
    w_o = din("w_o", [D, D])
    w_up = din("w_up", [D, 4 * D])
    w_dn = din("w_dn", [4 * D, D])
    c_cos = din("c_cos", [128, 33, 32])
    c_sin = din("c_sin", [128, 33, 32])
    c_mask = din("c_mask", [128, 9, 128])
    c_id = din("c_id", [128, 128])
    c_tri = din("c_tri", [128, 2, 128])
    c_tri1 = din("c_tri1", [128, 128])
    c_E = din("c_E", [128, 12, 128])

    yp = dout("yp", [SEQ, D])
    ys = dout("ys", [4, D])
    pkv = [(dout(f"p{g}k", [GROUPS[g][0], 256]), dout(f"p{g}v", [GROUPS[g][0], 256])) for g in range(3)]
    pgla = dout("pgla", [4, 128, 256])
    skv = [(dout(f"s{g}k", [4, GROUPS[g][0], 256]), dout(f"s{g}v", [4, GROUPS[g][0], 256])) for g in range(3)]
    sgla = dout("sgla", [4, 4, 128, 256])


    es = ExitStack()
    with es:
        k = KB(nc, es)
        global LAST_K
        LAST_K = k

        xres = Buf(k, "xres", [128, 4, D], F32, nres=4)
        actTs = [Buf(k, f"actT{i}", [128, 8, 512], BF16, nres=4) for i in range(2)]
        actT = actTs[0]
        xst = Buf(k, "xst", [128, D], F32)
        oaT = Buf(k, "oaT", [128, 2, 512], BF16, nres=4)
        obT = Buf(k, "obT", [128, 8, 512], BF16, nres=4)
        mixT = Buf(k, "mixT", [128, 8, 512], BF16, nres=8)
        fT = mixT
        qbT = Buf(k, "qbT", [128, 4, 512], BF16, nres=4)
        kbT = Buf(k, "kbT", [128, 4, 512], BF16, nres=4)
        glrT = Buf(k, "glrT", [32, 512], BF16)
        slabs = [Buf(k, f"slab{i}", [128, 8, 512], BF16) for i in range(NSLAB)]
        Sst = Buf(k, "Sst", [128, 4, 256], F32, nres=4)
        ident = Buf(k, "ident", [128, 128], BF16, const=True)
        masks = Buf(k, "masks", [128, 9, 128], BF16, const=True)
        cosb = Buf(k, "cosb", [128, 4, 32], F32)
        sinb = Buf(k, "sinb", [128, 4, 32], F32)
        g1b = Buf(k, "g1b", [128, 8], F32, const=True)
        g2b = Buf(k, "g2b", [128, 8], F32, const=True)
        gfb = Buf(k, "gfb", [128, D], F32, const=True)
        ggb = Buf(k, "ggb", [128, 256], F32, const=True)
        wgk = Buf(k, "wgkb", [17, 512], BF16, const=True)
        trib = Buf(k, "trib", [128, 2, 128], F32, const=True)
        tri1 = Buf(k, "tri1", [128, 128], F32, const=True)
        onec = Buf(k, "onec", [128, 8], F32, const=True)
        junk = Buf(k, "junk", [128, D], BF16)
        ss = Rot([Buf(k, f"ss{i}", [128, 8], F32) for i in range(3)])
        ubf = Rot([Buf(k, f"ubf{i}", [128, D], BF16) for i in range(2)])
        ropet = [Buf(k, f"ropet{i}", [128, 8, 32], F32) for i in range(4)]
        qkr = Rot([Buf(k, f"qkr{i}", [128, 512], F32) for i in range(2)])
        vf = Rot([Buf(k, f"vf{i}", [128, 256], F32) for i in range(2)])
        rec = Rot([Buf(k, f"rec{i}", [128, 4], F32) for i in range(2)])
        oab = Rot([Buf(k, f"oab{i}", [128, 256], BF16) for i in range(2)])
        esb = Buf(k, "esb", [128, 512], F32)
        spb = esb
        e1 = Buf(k, "e1", [128, 4, 128], F32)
        nlast = Buf(k, "nlast", [128, 4], F32)
        vbs = Buf(k, "vbs", [128, D], BF16)
        vbs2 = [vbs, Buf(k, "vbsB", [128, D], BF16)]
        dec2 = [Buf(k, f"dec{i}", [128, 4], F32) for i in range(2)]
        gsb = Buf(k, "gsb", [128, D], BF16)
        obb = Buf(k, "obb", [128, D], BF16)
        sga = Rot([Buf(k, f"sga{i}", [128, 512], BF16) for i in range(2)])
        tt = Rot([Buf(k, f"tt{i}", [128, 512], BF16) for i in range(2)])
        banks = [Buf(k, f"pb{i}", [128, 512], F32, psum=True) for i in range(8)]
        mmP4 = Rot(banks[0:4])
        mmP6 = Rot(banks[0:6])
        mmP5 = Rot(banks[0:5])
        accP1 = Rot(banks[5:6])
        accP2 = Rot(banks[4:6])
        mmP = mmP4
        accP = Rot(banks[4:6])
        trP2 = Rot(banks[6:8])
        trP3 = Rot(banks[5:8])
        trP = trP2
        es_p = ExitStack()
        kTr = [Buf(k, f"kTr{g}", [128, 2, RING[g] * 128], BF16, nres=RING[g], es=es_p) for g in range(3)]
        vAr = [Buf(k, f"vAr{g}", [128, RINGV[g], 4, 65], BF16, nres=RINGV[g], es=es_p) for g in range(3)]
        qT = Buf(k, "qT", [128, 12, 128], BF16, nres=3, es=es_p)
        Sbf = Buf(k, "Sbf", [128, 4, 256], BF16, es=es_p)
        qkb = Rot([Buf(k, f"qkb{i}", [128, 512], BF16, es=es_p) for i in range(3)])
        pTs = Rot([Buf(k, f"pT{i}", [128, 512], BF16, es=es_p) for i in range(4)])
        e2 = Buf(k, "e2", [128, 4, 128], BF16, es=es_p)
        e3 = Buf(k, "e3", [128, 4, 128], BF16, es=es_p)
        qd = Buf(k, "qd", [128, 4, 128], BF16, es=es_p)
        qd2 = [qd, Buf(k, "qdB", [128, 4, 128], BF16, es=es_p)]
        kd = Buf(k, "kd", [128, 4, 128], BF16, es=es_p)
        klT = Buf(k, "klT", [128, 4, 128], BF16, es=es_p)
        kls = Buf(k, "kls", [128, 4, 128], BF16, es=es_p)
        kls2 = [kls, Buf(k, "klsB", [128, 4, 128], BF16, es=es_p)]
        ats = Buf(k, "ats", [128, 4, 128], BF16, es=es_p)
        ats2 = [ats, Buf(k, "atsB", [128, 4, 128], BF16, es=es_p)]

        r_win, r_wpa, r_wpb, r_wo, r_wup, r_wdn = (Res(n) for n in ("win", "wpa", "wpb", "wo", "wup", "wdn"))
        r_out = Res("outs")

        def cast_dram(dst, src, res, rows, cols, clo=0):
            rs = 128
            cs = 2048
            if not DBG["setup"] & 1:
                return
            for r0 in range(0, rows, rs):
                for c0 in range(clo, cols, cs):
                    c1 = min(cols, c0 + cs)
                    k.dma("pool", dst[r0:r0 + rs, c0:c1], src[r0:r0 + rs, c0:c1], res, wnodep=[res])

        r_const = Res("consts")
        r_const2 = Res("consts_sw")
        cbufs = []

        def ld(buf, src, eng="sp"):
            if not DBG["setup"] & 4:
                return
            rc_ = r_const if eng == "sp" else r_const2
            k.dma(eng, buf.t[:], src, rc_, wnodep=[rc_])
            cbufs.append((buf, rc_))

        ld(ident, c_id, "pool")
        ld(masks, c_mask, "pool")
        ld(wgk, wgk_d, "pool")
        ld(g1b, g1_d)
        ld(g2b, g2_d)
        ld(gfb, gf_d)
        ld(ggb, gg_d)
        ld(trib, c_tri)
        ld(tri1, c_tri1)
        for b_, rc_ in cbufs:
            b_.r.w = rc_.w

        if DBG["setup"] & 8:
            k.finish()
            es_p.close()
            return nc
        r_cp = Res("cachecp")

        cc_list = []
        for g in range(3):
            W = GROUPS[g][0]
            for kv in range(2):
                for b in range(4):
                    cc_list.append((skv[g][kv][b, 0:W - 16, :], cache[g][kv][b, 1:W - 15, :]))
                    cc_list.append((skv[g][kv][b, W - 16:W - 1, :], cache[g][kv][b, W - 15:W, :]))
        cc_big = [c for c in cc_list if c[0].shape[0] > 1000]
        cc_small = [c for c in cc_list if c[0].shape[0] <= 1000]

        def cache_copies(blk):
            todo = [cc_big[blk]] + cc_small[blk * 5:(blk + 1) * 5]
            for (o_, i_) in todo:
                k.dma("sp", o_, i_, r_cp, store=True)
        k.op("pool", lambda e: e.memset(onec.t[:], 1.0), writes=[onec.r])
        k.op("pool", lambda e: e.memset(glrT.t[:], 1.0), writes=[glrT.r])
        for g in range(3):
            k.op("pool", lambda e, g=g: e.memset(vAr[g].t[:], 1.0), writes=vAr[g].res)
        k.op("pool", lambda e: e.memset(Sst.t[:], 0.0), writes=Sst.res)
        k.op("pool", lambda e: e.memset(Sbf.t[:], 0.0), writes=[Sbf.r])
        k.op("pool", lambda e: e.memset(qT.t[:], 0.0), writes=qT.res)

        WSRC = dict(win=w_in, wpa=w_pa, wpb=w_pb, wo=w_o, wup=w_up, wdn=w_dn)

        def mk_uses():
            u = []
            for g in range(3):
                u.append([("win", 0, 8, OFF["qa"] + 256 * g, 256, 0), ("win", 0, 8, OFF["ka"] + 256 * g, 256, 256)])
            u.append([("win", 0, 8, OFF["va"], 512, 0)])
            u.append([("win", 0, 8, OFF["va"] + 512, 256, 0)])
            u.append([("win", 0, 8, OFF["qb"], 512, 0)])
            u.append([("win", 0, 8, OFF["kb"], 512, 0)])
            u.append([("win", 0, 8, OFF["glr"], 16, 0)])
            u.append([("win", 0, 8, OFF["vb"], 512, 0)])
            u.append([("win", 0, 8, OFF["vb"] + 512, 512, 0)])
            u.append([("win", 0, 8, OFF["rb"], 512, 0)])
            u.append([("win", 0, 8, OFF["rb"] + 512, 512, 0)])
            for s_ in range(2):
                u.append([("win", 0, 8, OFF["ga"] + 512 * s_, 512, 0)])
                u.append([("win", 0, 8, OFF["gb"] + 512 * s_, 512, 0)])
                u.append([("wpa", 0, 2, 512 * s_, 512, 0)])
                u.append([("wpb", 0, 8, 512 * s_, 512, 0)])
            for s_ in range(2):
                u.append([("wo", 0, 8, 512 * s_, 512, 0)])
            for q in range(4):
                u.append([("wup", 0, 8, 1024 * q, 512, 0)])
                u.append([("wup", 0, 8, 1024 * q + 512, 512, 0)])
                u.append([("wdn", 1024 * q, 8, 0, 512, 0)])
                u.append([("wdn", 1024 * q, 8, 512, 512, 0)])
            return u

        utypes = mk_uses()
        NU = len(utypes)
        wsl = dint("wsl", [NU, 128, 8 * 512], BF16)
        cgroups = [(0, 5), (5, 12), (12, 24), (24, NU)]
        cres = [Res(f"cast{i}") for i in range(len(cgroups))]
        ures = [None] * NU
        for gi, (a_, b_) in enumerate(cgroups):
            for v in range(a_, b_):
                ures[v] = cres[gi]
                for (wk, r0, nkc, c0, w, off) in utypes[v]:
                    if DBG["setup"] & 1:
                        k.dma("pool", wsl[v].rearrange("p (k n) -> p k n", k=8)[:, 0:nkc, off:off + w],
                              WSRC[wk][r0:r0 + nkc * 128, c0:c0 + w].rearrange("(k p) n -> p k n", p=128),
                              cres[gi], wnodep=[cres[gi]])

        NBLK = 9
        uses = []
        for _ in range(NBLK):
            uses += list(range(NU))

        class Slabs:
            def __init__(self):
                self.loaded = 0
                self.nxt = 0
                self.released = set()

            def pump(self):
                while self.loaded < len(uses) and (self.loaded < NSLAB or (self.loaded - NSLAB) in self.released):
                    v = self.loaded
                    b = slabs[v % NSLAB]
                    ut = uses[v]
                    nkc = utypes[ut][0][2]
                    wtot = max(off + w for (_, _, _, _, w, off) in utypes[ut])
                    if wtot == 512:
                        k.dma("sp", b.t[:, 0:nkc, :].rearrange("p k n -> p (k n)"), wsl[ut][:, 0:nkc * 512],
                              b.r, reads=[ures[ut]], writes=[b.r])
                    else:
                        k.dma("sp", b.t[:, 0:nkc, 0:wtot], wsl[ut].rearrange("p (k n) -> p k n", k=8)[:, 0:nkc, 0:wtot],
                              b.r, reads=[ures[ut]], writes=[b.r])
                    self.loaded += 1

            def get(self):
                u = self.nxt
                self.nxt += 1
                self.pump()
                assert self.loaded > u, "slab ring deadlock"
                return u, slabs[u % NSLAB]

            def rel(self, u):
                self.released.add(u)
                self.pump()

        SL = Slabs()

        cur = {"blk": 0}

        def PL():
            return "dve" if cur["blk"] == 0 else "pool"

        def mm_acc(ps_ap, pairs, reads, psres):
            def fn(e):
                ins = None
                n = len(pairs)
                for i, (l, r) in enumerate(pairs):
                    ins = e.matmul(ps_ap, l, r, start=(i == 0), stop=(i == n - 1))
                return ins
            k.op("pe", fn, reads=reads, writes=[psres])

        def transposes(src_buf, src_ap_fn, n, dst_fn, dst_res=None):
            for c0 in range(0, n, 4):
                m = min(4, n - c0)
                pb = trP.next()

                def fn(e, c0=c0, m=m, pb=pb):
                    ins = None
                    for lc in range(m):
                        ins = e.matmul(pb.t[:, lc * 128:(lc + 1) * 128], src_ap_fn(c0 + lc), ident.t[:, :], start=True, stop=True)
                    return ins
                k.op("pe", fn, reads=[src_buf.r, ident.r], writes=[pb.r])
                dst_fn(pb.t[:, :], pb, c0)

        def cp3(eng, out3, in3, pb, dst_res):
            if eng == "act":
                k.op("act", lambda e: e.activation(out=out3, in_=in3, func=AF.Copy), reads=[pb.r], writes=[dst_res])
            else:
                k.op("dve", lambda e: e.tensor_copy(out=out3, in_=in3), reads=[pb.r], writes=[dst_res])

        def norm_tile(x_ap, xr, gb, out_ap, outr, feat_gain=False):
            s = ss.next()
            k.op("act", lambda e: e.activation(out=junk.t[:, :], in_=x_ap, func=AF.Square, scale=1.0 / 32.0,
                                               accum_out=s.t[:, 0:1]), reads=[xr], writes=[junk.r, s.r])
            k.op("act", lambda e: e.activation(out=s.t[:, 4:5], in_=s.t[:, 0:1], func=AF.Ln, bias=EPS), reads=[s.r], writes=[s.r])
            k.op("act", lambda e: e.activation(out=s.t[:, 1:2], in_=s.t[:, 4:5], func=AF.Exp, scale=-0.5), reads=[s.r], writes=[s.r])
            if feat_gain:
                k.op("dve", lambda e: e.tensor_scalar(out=out_ap, in0=x_ap, scalar1=s.t[:, 1:2], scalar2=None, op0=ALU.mult),
                     reads=[xr, s.r], writes=[outr])
            else:
                assert gb is not None
                k.op("dve", lambda e: e.scalar_tensor_tensor(out=out_ap, in0=x_ap, scalar=s.t[:, 1:2], in1=gb.t[:, :],
                                                             op0=ALU.mult, op1=ALU.mult), reads=[xr, s.r, gb.r], writes=[outr])

        def norm_phase(x_ap, xr):
            u = ubf.next()
            norm_tile(x_ap, xr, None, u.t[:, :], u.r, feat_gain=True)
            return u

        def xpose_phase(u, j, gb, dbuf):
            def dst(pv, pb, c0):
                for lc in range(4):
                    c = c0 + lc
                    src = pv[:, lc * 128:(lc + 1) * 128]
                    if lc % 2 == 0:
                        k.op("act", lambda e: e.mul(out=dbuf.t[:, c, j * 128:(j + 1) * 128], in_=src, mul=gb.t[:, c:c + 1]),
                             reads=[pb.r, gb.r], writes=[dbuf.res[j]])
                    else:
                        k.op("dve", lambda e: e.tensor_scalar(out=dbuf.t[:, c, j * 128:(j + 1) * 128], in0=src,
                                                              scalar1=gb.t[:, c:c + 1], scalar2=None, op0=ALU.mult),
                             reads=[pb.r, gb.r], writes=[dbuf.res[j]])
            transposes(u, lambda c: u.t[:, c * 128:(c + 1) * 128], 8, dst, None)

        def to_actT(j, x_ap, xr, gb):
            u = norm_phase(x_ap, xr)
            xpose_phase(u, j, gb, actT)

        def proj_tok(j, slab, width, nkc=8, src=None, srcres=None):
            src = actT if src is None else src
            pb = mmP.next()
            pairs = [(src.t[:, kc, j * 128:(j + 1) * 128], slab.t[:, kc, 0:width]) for kc in range(nkc)]
            mm_acc(pb.t[:, 0:width], pairs, [src.res[j], slab.r], pb.r)
            return pb

        def proj_feat(slab, lc, N, ntile, nkc=8, src=None, rows=128):
            src = actT if src is None else src
            pb = mmP.next()
            pairs = [(slab.t[:, kc, lc * 128:lc * 128 + rows], src.t[:, kc, 0:N]) for kc in range(nkc)]
            mm_acc(pb.t[0:rows, 0:N], pairs, [src.res[j] for j in range(ntile)] + [slab.r], pb.r)
            return pb

        def run_block(blk):
            sample = blk == 8
            cur["blk"] = blk
            ntile = 1 if sample else 4
            N = ntile * 128
            T0 = blk * 4

            k.tag = "s0"
            nonlocal actT
            actT = actTs[blk % 2]
            ctile = 32 if sample else T0
            k.dma("sp", cosb.t[:, 0:ntile, :], c_cos[:, ctile:ctile + ntile, :], cosb.r, writes=[cosb.r])
            k.dma("sp", sinb.t[:, 0:ntile, :], c_sin[:, ctile:ctile + ntile, :], sinb.r, writes=[sinb.r])
            for j in range(ntile):
                if sample:
                    k.op("pool", lambda e: e.memset(xres.t[:, 0, :], 0.0), writes=[xres.res[0]])
                    k.dma("sp", xres.t[0:4, 0, :], xs, xres.res[0], writes=[xres.res[0]])
                else:
                    t0 = (T0 + j) * 128
                    k.dma("sp", xres.t[:, j, :], xp[t0:t0 + 128, :], xres.res[j], writes=[xres.res[j]])
                if blk == 0:
                    to_actT(j, xres.t[:, j, :], xres.res[j], g1b)

            if DBG["stop"] == "s0":
                return
            nonlocal mmP, accP, trP
            trP = trP2
            if not sample:
                mmP, accP = mmP5, accP1
            uqk = [SL.get() for _ in range(3)]
            uv = [SL.get() for _ in range(2)]

            def produce(j, g):
                k.tag = "A.prod"
                T = 32 if sample else T0 + j
                W, dil = GROUPS[g]
                slot = T % RING[g]
                vslot = T % RINGV[g]
                pb = proj_tok(j, uqk[g][1], 512)
                x4 = pb.t[:, 0:512].rearrange("p (h two i) -> p h two i", two=2, i=32)
                x1, x2 = x4[:, :, 0, :], x4[:, :, 1, :]
                cb = cosb.t[:, j, :].unsqueeze(1).to_broadcast([128, 8, 32])
                sb = sinb.t[:, j, :].unsqueeze(1).to_broadcast([128, 8, 32])
                if sample:
                    qr_ap, qr_res = sqk.t[:, g, :], sqk.res[g]
                else:
                    qr = qkr.next()
                    qr_ap, qr_res = qr.t[:, :], qr.r
                o4 = qr_ap.rearrange("p (h two i) -> p h two i", two=2, i=32)
                t1, t2, t3, t4 = ropet
                k.op("dve", lambda e: e.tensor_tensor(out=t1.t[:], in0=x1, in1=cb, op=ALU.mult), reads=[pb.r, cosb.r], writes=[t1.r])
                k.op("dve", lambda e: e.tensor_tensor(out=t2.t[:], in0=x2, in1=sb, op=ALU.mult), reads=[pb.r, sinb.r], writes=[t2.r])
                k.op("dve", lambda e: e.tensor_tensor(out=o4[:, :, 0, :], in0=t1.t[:], in1=t2.t[:], op=ALU.subtract),
                     reads=[t1.r, t2.r], writes=[qr_res])
                k.op("dve", lambda e: e.tensor_tensor(out=t3.t[:], in0=x2, in1=cb, op=ALU.mult), reads=[pb.r, cosb.r], writes=[t3.r])
                k.op("dve", lambda e: e.tensor_tensor(out=t4.t[:], in0=x1, in1=sb, op=ALU.mult), reads=[pb.r, sinb.r], writes=[t4.r])
                k.op(PL(), lambda e: e.tensor_tensor(out=o4[:, :, 1, :], in0=t3.t[:], in1=t4.t[:], op=ALU.add),
                     reads=[t3.r, t4.r], writes=[qr_res])
                vsl = uv[0][1] if g < 2 else uv[1][1]
                vc0 = (g % 2) * 256 if g < 2 else 0
                pv_ = mmP.next()
                pairs = [(actT.t[:, kc, j * 128:(j + 1) * 128], vsl.t[:, kc, vc0:vc0 + 256]) for kc in range(8)]
                mm_acc(pv_.t[:, 0:256], pairs, [actT.res[j], vsl.r], pv_.r)
                if sample:
                    k.op("act", lambda e: e.activation(out=svv.t[:, g, :], in_=pv_.t[:, 0:256], func=AF.Copy),
                         reads=[pv_.r], writes=[svv.res[g]])
                    return None
                tout = T - (NT - W // 128)
                if tout >= 0:
                    k.dma("sp", pkv[g][0][tout * 128:(tout + 1) * 128, :], qr_ap[:, 256:512], qr_res,
                          reads=[qr_res], store=True)
                    vfb = vf.next()
                    k.op("act", lambda e: e.activation(out=vfb.t[:, 0:256], in_=pv_.t[:, 0:256], func=AF.Copy),
                         reads=[pv_.r], writes=[vfb.r])
                    k.dma("sp", pkv[g][1][tout * 128:(tout + 1) * 128, :], vfb.t[:, 0:256], vfb.r,
                          reads=[vfb.r], store=True)
                k.op("act", lambda e: e.activation(out=vAr[g].t[:, vslot, :, 0:64],
                                                   in_=pv_.t[:, 0:256].rearrange("p (h d) -> p h d", d=64), func=AF.Copy),
                     reads=[pv_.r], writes=[vAr[g].res[vslot]])
                qb_ = qkb.next()
                if cur["blk"] == 0:
                    k.op("act", lambda e: e.activation(out=qb_.t[:, :], in_=qr_ap, func=AF.Copy), reads=[qr_res], writes=[qb_.r])
                else:
                    k.op("pool", lambda e: e.tensor_copy(out=qb_.t[:, :], in_=qr_ap), reads=[qr_res], writes=[qb_.r])
                return (j, g, slot, qb_)

            def xpose(ctx):
                k.tag = "A.xpose"
                j, g, slot, qb_ = ctx

                def dst(pv, pb2, c0):
                    for hh in range(2):
                        cp3("act", qT.t[hh * 64:(hh + 1) * 64, 4 * g + hh:4 * g + 4:2, :],
                            pv[hh * 64:(hh + 1) * 64, 0:256].rearrange("p (c t) -> p c t", c=2), pb2, qT.res[g])
                    cp3("act", kTr[g].t[:, :, slot * 128:(slot + 1) * 128], pv[:, 256:512].rearrange("p (c t) -> p c t", c=2),
                        pb2, kTr[g].res[slot])
                transposes(qb_, lambda c: qb_.t[:, c * 128:(c + 1) * 128], 4, dst)

            def attention(j):
                k.tag = "A.attn"
                T = T0 + j
                ab = accP.next()
                units = []
                for g in range(3):
                    W, dil = GROUPS[g]
                    nd = W // 128
                    for dlt in range(0, min(T, nd) + 1):
                        kind = 0 if dlt == 0 else (2 if dlt == nd else 1)
                        units.append((g, (T - dlt) % RING[g], kind, (T - dlt) % RINGV[g]))
                nu = len(units)

                def emit_qk(ui):
                    g, slot, kind, vslot = units[ui]
                    sp_ = mmP.next()

                    def fqk(e):
                        ins = None
                        for c in range(2):
                            ins = e.matmul(sp_.t[:, c * 256:(c + 1) * 256],
                                           kTr[g].t[:, c, slot * 128:(slot + 1) * 128],
                                           qT.t[:, 4 * g + 2 * c:4 * g + 2 * c + 2, :].rearrange("p h t -> p (h t)"), start=True, stop=True)
                        return ins
                    k.op("pe", fqk, reads=[kTr[g].res[slot], qT.res[g]], writes=[sp_.r])
                    return sp_
                sps = {}
                for ui in range(min(3, nu)):
                    sps[ui] = emit_qk(ui)
                for ui in range(nu):
                    g, slot, kind, vslot = units[ui]
                    sp_ = sps.pop(ui)
                    pt = pTs.next()
                    k.op("act", lambda e: e.activation(out=pt.t[:, :], in_=sp_.t[:, :], func=AF.Exp, scale=0.125),
                         reads=[sp_.r], writes=[pt.r])
                    p3 = pt.t[:, :].rearrange("p (h t) -> p h t", h=4)
                    mk = masks.t[:, 3 * g + kind, :].unsqueeze(1).to_broadcast([128, 4, 128])
                    k.op("dve", lambda e: e.tensor_tensor(out=p3, in0=p3, in1=mk, op=ALU.mult),
                         reads=[pt.r, masks.r], writes=[pt.r])
                    if ui + 3 < nu:
                        sps[ui + 3] = emit_qk(ui + 3)

                    def fpv(e):
                        ins = None
                        for h in range(4):
                            ins = e.matmul(ab.t[:, h * 65:(h + 1) * 65], pt.t[:, h * 128:(h + 1) * 128],
                                           vAr[g].t[:, vslot, h, :], start=(ui == 0 and h == 0), stop=(ui == nu - 1),
                                           skip_group_check=True)
                        return ins
                    k.op("pe", fpv, reads=[pt.r, vAr[g].res[vslot]], writes=[ab.r])
                a3 = ab.t[:, 0:260].rearrange("p (h e) -> p h e", e=65)
                rc = rec.next()
                k.op("dve", lambda e: e.reciprocal(out=rc.t[:, :].unsqueeze(2), in_=a3[:, :, 64:65]), reads=[ab.r], writes=[rc.r])
                ob_ = oab.next()
                k.op("dve", lambda e: e.tensor_tensor(out=ob_.t[:, :].rearrange("p (h d) -> p h d", d=64), in0=a3[:, :, 0:64],
                                                      in1=rc.t[:, :].unsqueeze(2).to_broadcast([128, 4, 64]), op=ALU.mult),
                     reads=[ab.r, rc.r], writes=[ob_.r])

                def dst(pv, pb2, c0):
                    cp3("act", oaT.t[:, :, j * 128:(j + 1) * 128], pv[:, 0:256].rearrange("p (c t) -> p c t", c=2), pb2, oaT.res[j])
                transposes(ob_, lambda c: ob_.t[:, c * 128:(c + 1) * 128], 2, dst)

            pending = []
            for j in range(ntile):
                for g in range(3):
                    ctx = produce(j, g)
                    if ctx is not None:
                        pending.append(ctx)
                    if len(pending) > 2:
                        c_ = pending.pop(0)
                        xpose(c_)
                        if c_[1] == 2:
                            attention(c_[0])
            while pending:
                c_ = pending.pop(0)
                xpose(c_)
                if c_[1] == 2:
                    attention(c_[0])
            if sample:
                sample_attention()
            for u_, _ in uqk + uv:
                SL.rel(u_)

            if DBG["stop"] == "A":
                return
            k.tag = "B"
            accP = accP2
            mmP = mmP4 if sample else mmP5
            trP = trP2 if sample else trP3
            for wi, (dstb, scale) in enumerate(((qbT, 128.0 ** -0.5), (kbT, 1.0))):
                u_, sl = SL.get()
                for h in range(4):
                    pb = proj_feat(sl, h, N, ntile)
                    k.op("act", lambda e: e.mul(out=dstb.t[:, h, 0:N], in_=pb.t[:, 0:N], mul=scale),
                         reads=[pb.r], writes=[dstb.res[h]])
                SL.rel(u_)
            u_, sl = SL.get()
            pb = proj_feat(sl, 0, N, ntile, rows=16)
            k.op("act", lambda e: e.activation(out=glrT.t[0:16, 0:N], in_=pb.t[0:16, 0:N], func=AF.Copy), reads=[pb.r], writes=[glrT.r])
            SL.rel(u_)
            uvb = [SL.get() for _ in range(2)]
            urb = [SL.get() for _ in range(2)]

            def prep_a(j):
                P = j % 2
                cs = slice(j * 128, (j + 1) * 128)
                qd_, ats_, kls_, vbs_, dec_ = qd2[P], ats2[P], kls2[P], vbs2[P], dec2[P]
                pb = mmP.next()
                mm_acc(pb.t[:, 0:512], [(glrT.t[0:17, cs], wgk.t[0:17, :])], [glrT.r, wgk.r], pb.r)
                k.op("act", lambda e: e.activation(out=esb.t[:, :], in_=pb.t[:, :], func=AF.Exp, scale=-1.0), reads=[pb.r], writes=[esb.r])
                k.op("act", lambda e: e.activation(out=spb.t[:, :], in_=esb.t[:, :], func=AF.Ln, bias=1.0), reads=[esb.r], writes=[spb.r])
                for half in range(2):
                    pv_ = proj_tok(j, uvb[half][1], 512)
                    k.op("act", lambda e: e.activation(out=vbs_.t[:, half * 512:(half + 1) * 512], in_=pv_.t[:, :], func=AF.Copy),
                         reads=[pv_.r], writes=[vbs_.r])
                    if sample:
                        k.op("dve", lambda e: e.tensor_copy(out=vbf.t[:, half * 512:(half + 1) * 512], in_=pv_.t[:, :]),
                             reads=[pv_.r], writes=[vbf.r])
                pc = mmP.next()

                def fcum(e):
                    ins = None
                    for h in range(4):
                        ins = e.matmul(pc.t[:, h * 128:(h + 1) * 128], spb.t[:, h * 128:(h + 1) * 128],
                                       trib.t[:, 1 if sample else 0, :], start=True, stop=True)
                    return ins
                k.op("pe", fcum, reads=[spb.r, trib.r], writes=[pc.r])
                pc3 = pc.t[:, :].rearrange("p (h t) -> p h t", h=4)
                k.op("act", lambda e: e.activation(out=e1.t[:], in_=pc3, func=AF.Exp, scale=-1.0), reads=[pc.r], writes=[e1.r])
                if not sample:
                    k.op("act", lambda e: e.activation(out=e2.t[:], in_=pc3, func=AF.Exp, scale=1.0), reads=[pc.r], writes=[e2.r])
                    k.op("dve", lambda e: e.tensor_copy(out=dec_.t[:, :].unsqueeze(2), in_=e1.t[:, :, 127:128]), reads=[e1.r], writes=[dec_.r])
                if sample:
                    return
                k.op("dve", lambda e: e.tensor_tensor(out=qd_.t[:], in0=e1.t[:], in1=qbT.t[:, :, cs], op=ALU.mult),
                     reads=[e1.r] + qbT.res, writes=[qd_.r])
                k.op("dve", lambda e: e.tensor_tensor(out=kd.t[:], in0=e2.t[:], in1=kbT.t[:, :, cs], op=ALU.mult),
                     reads=[e2.r] + kbT.res, writes=[kd.r])
                for h in range(4):
                    k.op("dve",
                         lambda e: e.scalar_tensor_tensor(out=klT.t[:, h, :], in0=e2.t[:, h, :], scalar=dec_.t[:, h:h + 1],
                                                          in1=kbT.t[:, h, cs], op0=ALU.mult, op1=ALU.mult),
                         reads=[e2.r, dec_.r, kbT.res[h]], writes=[klT.r])

            def prep_b(j):
                if sample:
                    return
                P = j % 2
                qd_, ats_, kls_, vbs_, dec_ = qd2[P], ats2[P], kls2[P], vbs2[P], dec2[P]

                def dst(pv, pb2, c0):
                    cp3("act", kls_.t[:], pv[:, 0:512].rearrange("p (h d) -> p h d", h=4), pb2, kls_.r)
                transposes(klT, lambda c: klT.t[:, c, :], 4, dst, None)
                pa_ = mmP.next()

                def fat(e):
                    ins = None
                    for h in range(4):
                        ins = e.matmul(pa_.t[:, h * 128:(h + 1) * 128], kd.t[:, h, :], qd_.t[:, h, :], start=True, stop=True)
                    return ins
                k.op("pe", fat, reads=[kd.r, qd_.r], writes=[pa_.r])
                k.op("dve", lambda e: e.tensor_tensor(out=ats_.t[:], in0=pa_.t[:, :].rearrange("p (h t) -> p h t", h=4),
                                                      in1=tri1.t[:, :].unsqueeze(1).to_broadcast([128, 4, 128]), op=ALU.mult),
                     reads=[pa_.r, tri1.r], writes=[ats_.r])

            def fin_a(j):
                for half in range(2):
                    pr_ = proj_tok(j, urb[half][1], 512)
                    k.op("act", lambda e: e.activation(out=gsb.t[:, half * 512:(half + 1) * 512], in_=pr_.t[:, :], func=AF.Silu),
                         reads=[pr_.r], writes=[gsb.r])
                g3 = gsb.t[:, :].rearrange("p (h v) -> p h v", h=4)
                k.op(PL(), lambda e: e.tensor_tensor(out=g3, in0=g3, in1=ggb.t[:, :].unsqueeze(1).to_broadcast([128, 4, 256]), op=ALU.mult),
                     reads=[gsb.r, ggb.r], writes=[gsb.r])

            def fin_b(j):
                P = j % 2
                T = T0 + j
                qd_, ats_, kls_, vbs_, dec_ = qd2[P], ats2[P], kls2[P], vbs2[P], dec2[P]
                if sample:
                    gla_sample()
                else:
                    for hb in range(2):
                        pbk = mmP.next()

                        def fo(e, hb=hb, pbk=pbk):
                            ins = None
                            for hh in range(2):
                                h = hb * 2 + hh
                                e.matmul(pbk.t[:, hh * 256:(hh + 1) * 256], ats_.t[:, h, :], vbs_.t[:, h * 256:(h + 1) * 256], start=(hh == 0), stop=False,
                                         skip_group_check=True)
                                ins = e.matmul(pbk.t[:, hh * 256:(hh + 1) * 256], qd_.t[:, h, :], Sbf.t[:, h, :], start=False, stop=True,
                                               skip_group_check=True)
                            return ins
                        k.op("pe", fo, reads=[ats_.r, vbs_.r, qd_.r, Sbf.r], writes=[pbk.r])
                        finish_o(hb, pbk)
                    for hb in range(2):
                        pbk = mmP.next()

                        def fs(e, hb=hb, pbk=pbk):
                            ins = None
                            for hh in range(2):
                                h = hb * 2 + hh
                                ins = e.matmul(pbk.t[:, hh * 256:(hh + 1) * 256], kls_.t[:, h, :], vbs_.t[:, h * 256:(h + 1) * 256], start=True, stop=True)
                            return ins
                        k.op("pe", fs, reads=[kls_.r, vbs_.r], writes=[pbk.r])
                        for hh in range(2):
                            h = hb * 2 + hh
                            k.op("dve", lambda e: e.scalar_tensor_tensor(out=Sst.t[:, h, :], in0=Sst.t[:, h, :], scalar=dec_.t[:, h:h + 1],
                                                                         in1=pbk.t[:, hh * 256:(hh + 1) * 256], op0=ALU.mult, op1=ALU.add),
                                 reads=[Sst.res[h], dec_.r, pbk.r], writes=[Sst.res[h]])
                            k.op(PL(), lambda e: e.tensor_copy(out=Sbf.t[:, h, :], in_=Sst.t[:, h, :]),
                                 reads=[Sst.res[h]], writes=[Sbf.r])
                    if T == NT - 1:
                        k.dma("sp", pgla.rearrange("h d v -> d h v"), Sst.t[:], Sst.res[0], reads=Sst.res, store=True)

            def late(j):
                def dst(pv, pb2, c0, j=j):
                    cp3("act" if c0 == 0 else "dve", obT.t[:, c0:c0 + 4, j * 128:(j + 1) * 128],
                        pv[:, 0:512].rearrange("p (c t) -> p c t", c=4), pb2, obT.res[j])
                transposes(obb, lambda c: obb.t[:, c * 128:(c + 1) * 128], 8, dst, None)

            prep_a(0)
            prep_b(0)
            if ntile > 1:
                prep_a(1)
            fin_a(0)
            if ntile > 1:
                prep_b(1)
            for j in range(ntile):
                fin_b(j)
                if j + 2 < ntile:
                    prep_a(j + 2)
                if j + 1 < ntile:
                    fin_a(j + 1)
                late(j)
                if j + 2 < ntile:
                    prep_b(j + 2)
            for u_, _ in uvb + urb:
                SL.rel(u_)

            if DBG["stop"] == "B":
                return
            k.tag = "C"
            mmP = mmP4
            trP = trP2
            if not sample:
                cache_copies(blk)
            for s in range(2):
                uga, ugb, upa, upb = SL.get(), SL.get(), SL.get(), SL.get()
                for lc in range(4):
                    c = s * 4 + lc
                    pga = proj_feat(uga[1], lc, N, ntile)
                    sa = sga.next()
                    k.op("act", lambda e: e.activation(out=sa.t[:, 0:N], in_=pga.t[:, 0:N], func=AF.Sigmoid), reads=[pga.r], writes=[sa.r])
                    pgb = proj_feat(ugb[1], lc, N, ntile)
                    sb_ = sga.next()
                    k.op("act", lambda e: e.activation(out=sb_.t[:, 0:N], in_=pgb.t[:, 0:N], func=AF.Sigmoid), reads=[pgb.r], writes=[sb_.r])
                    ppa = proj_feat(upa[1], lc, N, ntile, nkc=2, src=oaT)
                    ta = tt.next()
                    k.op("dve", lambda e: e.tensor_tensor(out=ta.t[:, 0:N], in0=ppa.t[:, 0:N], in1=sa.t[:, 0:N], op=ALU.mult),
                         reads=[ppa.r, sa.r], writes=[ta.r])
                    ppb = proj_feat(upb[1], lc, N, ntile, src=obT)
                    tb = tt.next()
                    k.op("dve", lambda e: e.tensor_tensor(out=tb.t[:, 0:N], in0=ppb.t[:, 0:N], in1=sb_.t[:, 0:N], op=ALU.mult),
                         reads=[ppb.r, sb_.r], writes=[tb.r])
                    k.op(PL(), lambda e: e.tensor_tensor(out=mixT.t[:, c, 0:N], in0=ta.t[:, 0:N], in1=tb.t[:, 0:N], op=ALU.add),
                         reads=[ta.r, tb.r], writes=[mixT.res[c]])
                for u_, _ in (uga, ugb, upa, upb):
                    SL.rel(u_)

            if DBG["stop"] == "C":
                return
            k.tag = "D"
            uwo = [SL.get(), SL.get()]
            dnorm = []
            for j in range(ntile):
                for nh in range(2):
                    sl = uwo[nh][1]
                    pb = mmP.next()
                    pairs = [(mixT.t[:, kc, j * 128:(j + 1) * 128], sl.t[:, kc, :]) for kc in range(8)]
                    mm_acc(pb.t[:, :], pairs, mixT.res + [sl.r], pb.r)
                    xa = xres.t[:, j, nh * 512:(nh + 1) * 512]
                    k.op("dve", lambda e: e.tensor_tensor(out=xa, in0=xa, in1=pb.t[:, :], op=ALU.add), reads=[pb.r, xres.res[j]], writes=[xres.res[j]])
                if j > 0:
                    dnorm.append((j - 1, norm_phase(xres.t[:, j - 1, :], xres.res[j - 1])))
                if len(dnorm) > 1:
                    jj, uu_ = dnorm.pop(0)
                    xpose_phase(uu_, jj, g2b, actT)
            dnorm.append((ntile - 1, norm_phase(xres.t[:, ntile - 1, :], xres.res[ntile - 1])))
            while dnorm:
                jj, uu_ = dnorm.pop(0)
                xpose_phase(uu_, jj, g2b, actT)
            for u_, _ in uwo:
                SL.rel(u_)

            if DBG["stop"] == "D":
                return
            k.tag = "E"
            nxt_sample = (blk == 7)
            do_pf = blk < 8
            for q in range(4):
                pf_u = None
                if do_pf and (not nxt_sample or q == 0):
                    if nxt_sample:
                        k.op("pool", lambda e: e.memset(xst.t[:, :], 0.0), writes=[xst.r])
                        k.dma("sp", xst.t[0:4, :], xs, xst.r, writes=[xst.r])
                    else:
                        t0n = (T0 + 4 + q) * 128
                        k.dma("sp", xst.t[:, :], xp[t0n:t0n + 128, :], xst.r, writes=[xst.r])
                    pf_u = True
                uu = [SL.get(), SL.get()]
                for fc in range(8):
                    if False and q == 0 and ntile == 4:
                        sl_ = uu[fc // 4][1]
                        lc_ = fc % 4
                        pb = mmP.next()
                        for hf in range(2):
                            prs = [(sl_.t[:, kc, lc_ * 128:(lc_ + 1) * 128], actT.t[:, kc, hf * 256:(hf + 1) * 256]) for kc in range(8)]

                            def fh(e, prs=prs, hf=hf, pb=pb):
                                ins = None
                                for i_, (l_, r_) in enumerate(prs):
                                    ins = e.matmul(pb.t[:, hf * 256:(hf + 1) * 256], l_, r_, start=(i_ == 0 and hf == 0), stop=(i_ == 7),
                                                   skip_group_check=True)
                                return ins
                            k.op("pe", fh, reads=[actT.res[2 * hf], actT.res[2 * hf + 1], sl_.r], writes=[pb.r])
                    else:
                        pb = proj_feat(uu[fc // 4][1], fc % 4, N, ntile)
                    k.op("act", lambda e: e.activation(out=fT.t[:, fc, 0:N], in_=pb.t[:, 0:N], func=AF.Relu), reads=[pb.r], writes=[fT.res[fc]])
                    k.op(PL() if fc % 2 == 0 and fc < 6 else "dve", lambda e: e.tensor_tensor(out=fT.t[:, fc, 0:N], in0=fT.t[:, fc, 0:N], in1=fT.t[:, fc, 0:N], op=ALU.mult),
                         reads=[fT.res[fc]], writes=[fT.res[fc]])
                for u_, _ in uu:
                    SL.rel(u_)
                if pf_u is not None:
                    pf_u = norm_phase(xst.t[:, :], xst.r)
                for nh in range(2):
                    u_, sl = SL.get()
                    for j in range(ntile):
                        pb = mmP.next()
                        pairs = [(fT.t[:, fc, j * 128:(j + 1) * 128], sl.t[:, fc, :]) for fc in range(8)]
                        if nh == 0 and j == 0:
                            for fc in range(8):
                                k.op("pe", lambda e: e.matmul(pb.t[:, :], pairs[fc][0], pairs[fc][1], start=(fc == 0), stop=(fc == 7)),
                                     reads=[fT.res[fc], sl.r], writes=[pb.r])
                        else:
                            mm_acc(pb.t[:, :], pairs, fT.res + [sl.r], pb.r)
                        xa = xres.t[:, j, nh * 512:(nh + 1) * 512]
                        k.op("dve", lambda e: e.tensor_tensor(out=xa, in0=xa, in1=pb.t[:, :], op=ALU.add), reads=[pb.r, xres.res[j]], writes=[xres.res[j]])
                    SL.rel(u_)
                if pf_u is not None:
                    k.tag = "pf"
                    xpose_phase(pf_u, q, g1b, actTs[(blk + 1) % 2])
                    k.tag = "E"
            for j in range(ntile):
                norm_tile(xres.t[:, j, :], xres.res[j], gfb, xres.t[:, j, :], xres.res[j])
                if sample:
                    k.dma("sp", ys, xres.t[0:4, j, :], xres.res[j], reads=[xres.res[j]], store=True)
                else:
                    t0 = (T0 + j) * 128
                    k.dma("sp", yp[t0:t0 + 128, :], xres.t[:, j, :], xres.res[j], reads=[xres.res[j]], store=True)

        def finish_o(hb, pbk):
            s = ss.next()
            for hh in range(2):
                k.op("act", lambda e: e.activation(out=junk.t[:, 0:256], in_=pbk.t[:, hh * 256:(hh + 1) * 256], func=AF.Square,
                                                   scale=1.0 / 16.0, accum_out=s.t[:, hh:hh + 1]), reads=[pbk.r], writes=[junk.r, s.r])
            k.op("act", lambda e: e.activation(out=s.t[:, 4:6], in_=s.t[:, 0:2], func=AF.Ln, bias=EPS), reads=[s.r], writes=[s.r])
            k.op("act", lambda e: e.activation(out=s.t[:, 2:4], in_=s.t[:, 4:6], func=AF.Exp, scale=-0.5), reads=[s.r], writes=[s.r])
            for hh in range(2):
                h = hb * 2 + hh
                k.op("dve", lambda e: e.scalar_tensor_tensor(out=obb.t[:, h * 256:(h + 1) * 256], in0=pbk.t[:, hh * 256:(hh + 1) * 256],
                                                             scalar=s.t[:, 2 + hh:3 + hh], in1=gsb.t[:, h * 256:(h + 1) * 256],
                                                             op0=ALU.mult, op1=ALU.mult), reads=[pbk.r, s.r, gsb.r], writes=[obb.r])

        def sample_attention():
            ab = accP.next()
            first = [True]

            def acc_mm(lhsT, rhs_buf, last):
                st = first[0]
                first[0] = False
                k.op("pe", lambda e: e.matmul(ab.t[:, 0:260], lhsT, rhs_buf.t[:].rearrange("p h e -> p (h e)"), start=st, stop=last),
                     reads=[rhs_buf.r, Eb.r], writes=[ab.r])
            for b in range(4):
                for g in range(3):
                    W, dil = GROUPS[g]
                    ck, cv = cache[g]
                    for (cin, cout, srcap, srcres) in ((ck, skv[g][0], sqk.t[b:b + 1, g, 256:512], sqk.res[g]),
                                                       (cv, skv[g][1], svv.t[b:b + 1, g, :], svv.res[g])):
                        rr = Res(f"cp{b}{g}{cout.name}")
                        k.dma("sp", cout[b, 0:W - 1, :], cin[b, 1:W, :], rr, store=True)
                        k.dma("sp", cout[b, W - 1:W, :], srcap, rr, reads=[srcres], store=True)
                    ks_ = ksel.next()
                    vs_ = vsel.next()
                    bcs = bcss.next()
                    vself = vselfs.next()
                    k.dma("sp", ks_.t[:, :], ck[b, 0:W:dil, :], ks_.r, writes=[ks_.r])
                    k.dma("sp", vs_.t[:, :, 0:64], cv[b, 0:W:dil, :].rearrange("p (h d) -> p h d", d=64), vs_.r, writes=[vs_.r])
                    pq = mmP.next()
                    k.op("pe", lambda e: e.matmul(pq.t[:, :], Eb.t[:, b, :], sqk.t[:, g, :], start=True, stop=True),
                         reads=[sqk.res[g], Eb.r], writes=[pq.r])
                    k.op("act", lambda e: e.activation(out=bcs.t[:, :], in_=pq.t[:, :], func=AF.Copy), reads=[pq.r], writes=[bcs.r])
                    pvv = mmP.next()
                    k.op("pe", lambda e: e.matmul(pvv.t[:, 0:256], Eb.t[:, b, :], svv.t[:, g, :], start=True, stop=True),
                         reads=[svv.res[g], Eb.r], writes=[pvv.r])
                    k.op("act", lambda e: e.activation(out=vself.t[:, :, 0:64], in_=pvv.t[:, 0:256].rearrange("p (h d) -> p h d", d=64), func=AF.Copy),
                         reads=[pvv.r], writes=[vself.r])
                    for which in range(2):
                        pr_ = prod.next()
                        kin = ks_.t[:, :] if which == 0 else bcs.t[:, 256:512]
                        kres = ks_.r if which == 0 else bcs.r
                        k.op("dve", lambda e: e.tensor_tensor(out=pr_.t[:, :], in0=kin, in1=bcs.t[:, 0:256], op=ALU.mult),
                             reads=[kres, bcs.r], writes=[pr_.r])
                        s4 = sc4.next()
                        k.op("dve", lambda e: e.reduce_sum(out=s4.t[:, :], in_=pr_.t[:, :].rearrange("p (h d) -> p h d", d=64), axis=AX.X),
                             reads=[pr_.r], writes=[s4.r])
                        k.op("act", lambda e: e.activation(out=s4.t[:, :], in_=s4.t[:, :], func=AF.Exp, scale=0.125), reads=[s4.r], writes=[s4.r])
                        pb_ = pvb.next()
                        vv = vs_ if which == 0 else vself
                        k.op("dve", lambda e: e.tensor_tensor(out=pb_.t[:], in0=vv.t[:], in1=s4.t[:, :].unsqueeze(2).to_broadcast([128, 4, 65]), op=ALU.mult),
                             reads=[vv.r, s4.r], writes=[pb_.r])
                        last = (b == 3 and g == 2 and which == 1)
                        acc_mm(Eb.t[:, (4 if which == 0 else 8) + b, :], pb_, last)
            a3 = ab.t[:, 0:260].rearrange("p (h e) -> p h e", e=65)
            rc = rec.next()
            k.op("dve", lambda e: e.tensor_scalar(out=rc.t[:, :].unsqueeze(2), in0=a3[:, :, 64:65], scalar1=1e-30, scalar2=None, op0=ALU.max),
                 reads=[ab.r], writes=[rc.r])
            k.op("dve", lambda e: e.reciprocal(out=rc.t[:, :], in_=rc.t[:, :]), reads=[rc.r], writes=[rc.r])
            ob_ = oab.next()
            k.op("dve", lambda e: e.tensor_tensor(out=ob_.t[:, :].rearrange("p (h d) -> p h d", d=64), in0=a3[:, :, 0:64],
                                                  in1=rc.t[:, :].unsqueeze(2).to_broadcast([128, 4, 64]), op=ALU.mult),
                 reads=[ab.r, rc.r], writes=[ob_.r])

            def dst(pv, pb2, c0):
                cp3("act", oaT.t[:, :, 0:128], pv[:, 0:256].rearrange("p (c t) -> p c t", c=2), pb2, oaT.res[0])
            transposes(ob_, lambda c: ob_.t[:, c * 128:(c + 1) * 128], 2, dst, None)

        def gla_sample():
            po = [accP.next(), accP.next()]
            for b in range(4):
                pvs = []
                for half in range(2):
                    pv_ = mmP.next()
                    k.op("pe", lambda e: e.matmul(pv_.t[:, :], Eb.t[:, b, :], vbf.t[:, half * 512:(half + 1) * 512], start=True, stop=True),
                         reads=[vbf.r, Eb.r], writes=[pv_.r])
                    pvs.append(pv_)
                for h in range(4):
                    si = sin_.next()
                    k.dma("sp", si.t[:, :], sg[b, h], si.r, writes=[si.r])
                    k.op("dve", lambda e: e.tensor_scalar(out=si.t[:, :], in0=si.t[:, :], scalar1=e1.t[:, h, b:b + 1], scalar2=None, op0=ALU.mult),
                         reads=[si.r, e1.r], writes=[si.r])
                    sn = snew.next()
                    pv_ = pvs[h // 2]
                    k.op("dve", lambda e: e.scalar_tensor_tensor(out=sn.t[:, :], in0=pv_.t[:, (h % 2) * 256:(h % 2 + 1) * 256],
                                                                 scalar=kbT.t[:, h, b:b + 1], in1=si.t[:, :], op0=ALU.mult, op1=ALU.add),
                         reads=[pv_.r, kbT.res[h], si.r], writes=[sn.r])
                    k.dma("sp", sgla[b, h], sn.t[:, :], sn.r, reads=[sn.r], store=True)
                    qs = qsel.next()
                    k.op("dve", lambda e: e.tensor_scalar(out=qs.t[:, :], in0=Eb.t[:, 4 + b, :], scalar1=qbT.t[:, h, b:b + 1], scalar2=None, op0=ALU.mult),
                         reads=[Eb.r, qbT.res[h]], writes=[qs.r])
                    pk_ = po[h // 2]
                    k.op("pe", lambda e: e.matmul(pk_.t[:, (h % 2) * 256:(h % 2 + 1) * 256], qs.t[:, :], sn.t[:, :], start=(b == 0 and h % 2 == 0), stop=(b == 3),
                                                  skip_group_check=True),
                         reads=[qs.r, sn.r], writes=[pk_.r])
            for hb in range(2):
                finish_o(hb, po[hb])
            return po

        for blk in range(DBG["nprompt"]):
            k.epoch = 1 + blk // 4
            run_block(blk)
        k.barrier()
        es_p.close()
        Eb = Buf(k, "Eb", [128, 12, 128], F32)
        k.dma("sp", Eb.t[:], c_E, Eb.r, writes=[Eb.r])
        sqk = Buf(k, "sqk", [128, 3, 512], F32, nres=3)
        svv = Buf(k, "svv", [128, 3, 256], F32, nres=3)
        vbf = Buf(k, "vbf", [128, D], F32)
        ksel = Rot([Buf(k, f"ksel{i}", [128, 256], F32) for i in range(3)])
        vsel = Rot([Buf(k, f"vsel{i}", [128, 4, 65], F32) for i in range(3)])
        vselfs = Rot([Buf(k, f"vself{i}", [128, 4, 65], F32) for i in range(2)])
        bcss = Rot([Buf(k, f"bcs{i}", [128, 512], F32) for i in range(2)])
        prod = Rot([Buf(k, f"prod{i}", [128, 256], F32) for i in range(4)])
        sc4 = Rot([Buf(k, f"sc4{i}", [128, 4], F32) for i in range(4)])
        pvb = Rot([Buf(k, f"pvb{i}", [128, 4, 65], F32) for i in range(4)])
        sin_ = Rot([Buf(k, f"sin{i}", [128, 256], F32) for i in range(3)])
        snew = Rot([Buf(k, f"snew{i}", [128, 256], F32) for i in range(3)])
        qsel = Rot([Buf(k, f"qsel{i}", [128, 128], F32) for i in range(3)])
        k.epoch = 3
        for vs in vsel.items + vselfs.items:
            k.op("pool", lambda e, vs=vs: e.memset(vs.t[:], 1.0), writes=[vs.r])
        if DBG["sample"]:
            run_block(NBLK - 1)
        k.finish()
    return nc


_NC = None


def _consts():
    half = 32
    inv = (10000.0 ** (-np.arange(half, dtype=np.float32) / half)).astype(np.float32)
    pos = np.zeros((128, 33), np.float32)
    for T in range(32):
        pos[:, T] = T * 128 + np.arange(128)
    pos[:, 32] = PAST
    ang = pos[:, :, None] * inv[None, None, :]
    c_cos = np.cos(ang).astype(np.float32)
    c_sin = np.sin(ang).astype(np.float32)
    p = np.arange(128)[:, None]
    f = np.arange(128)[None, :]
    m = np.zeros((128, 9, 128), np.float32)
    for g, (W, d) in enumerate(GROUPS):
        mod = ((f - p) % d) == 0
        m[:, 3 * g + 0, :] = mod & (f >= p)
        m[:, 3 * g + 1, :] = mod
        m[:, 3 * g + 2, :] = mod & (f <= p)
    tri = (p <= f).astype(np.float32)
    c_tri = np.zeros((128, 2, 128), np.float32)
    c_tri[:, 0, :] = tri / 16.0
    c_tri[:, 1, :] = np.eye(128, dtype=np.float32) / 16.0
    E = np.zeros((128, 12, 128), np.float32)
    for b in range(4):
        E[b, b, :] = 1.0
        E[:, 4 + b, b] = 1.0
        E[0, 8 + b, b] = 1.0
    return dict(c_cos=c_cos, c_sin=c_sin, c_mask=m, c_id=np.eye(128, dtype=np.float32), c_tri=c_tri, c_tri1=tri, c_E=E)


def kernel(x_prompt, x_sample, cache_a1_k, cache_a1_v, cache_a2_k, cache_a2_v, cache_a3_k, cache_a3_v,
           state_gla, g_norm1, w_in, w_gk2, b_gk, g_gla, w_pa, w_pb, w_o, g_norm2, w_up, w_down, g_final):
    global _NC
    f = lambda a: np.ascontiguousarray(np.asarray(a, dtype=np.float32))
    if _NC is None:
        _NC = build()
    nc = _NC
    cs = _consts()
    caches = [(f(cache_a1_k)[0], f(cache_a1_v)[0]), (f(cache_a2_k)[0], f(cache_a2_v)[0]), (f(cache_a3_k)[0], f(cache_a3_v)[0])]
    xp_, xs_ = f(x_prompt), f(x_sample)
    sg_ = f(state_gla)[0]
    rep = lambda v: np.ascontiguousarray(np.broadcast_to(f(v).reshape(1, -1), (128, f(v).size)))
    gT = lambda v: np.ascontiguousarray(f(v).reshape(8, 128).T)
    shared = dict(g1=gT(g_norm1), g2=gT(g_norm2), gf=rep(g_final), gg=rep(g_gla), w_in=f(w_in)[0],
                  wgk=np.ascontiguousarray(np.concatenate([f(w_gk2)[0], f(b_gk)[0][None, :]], axis=0)),
                  w_pa=f(w_pa)[0], w_pb=f(w_pb)[0], w_o=f(w_o)[0], w_up=f(w_up)[0], w_dn=f(w_down)[0], **cs)
    in_maps = []
    for c in range(8):
        m = dict(shared)
        m["xp"] = xp_[c]
        m["xs"] = np.ascontiguousarray(xs_[4 * c:4 * c + 4, 0, :])
        for g in range(3):
            W = GROUPS[g][0]
            m[f"c{g}k"] = np.ascontiguousarray(caches[g][0][4 * c:4 * c + 4].reshape(4, W, 256))
            m[f"c{g}v"] = np.ascontiguousarray(caches[g][1][4 * c:4 * c + 4].reshape(4, W, 256))
        m["sg"] = np.ascontiguousarray(sg_[4 * c:4 * c + 4])
        in_maps.append(m)
    ncr = DBG["ncores"]
    res = run_bass_kernel_spmd(nc, in_maps[:ncr], core_ids=list(range(ncr)))
    R = list(res.results) + [res.results[0]] * (8 - ncr)
    y_prompt = np.stack([R[c]["yp"] for c in range(8)], axis=0)
    y_sample = np.concatenate([R[c]["ys"] for c in range(8)], axis=0).reshape(32, 1, D)
    outs = [y_prompt, y_sample]
    for g in range(3):
        W = GROUPS[g][0]
        for kv in ("k", "v"):
            outs.append(np.stack([R[c][f"p{g}{kv}"] for c in range(8)], axis=0).reshape(1, 8, W, 4, 64))
    outs.append(np.stack([R[c]["pgla"] for c in range(8)], axis=0).reshape(1, 8, 4, 128, 256))
    for g in range(3):
        W = GROUPS[g][0]
        for kv in ("k", "v"):
            outs.append(np.concatenate([R[c][f"s{g}{kv}"] for c in range(8)], axis=0).reshape(1, 32, W, 4, 64))
    outs.append(np.concatenate([R[c]["sgla"] for c in range(8)], axis=0).reshape(1, 32, 4, 128, 256))
    return tuple(np.ascontiguousarray(o.astype(np.float32)) for o in outs)
```

```python
import numpy as np
from contextlib import ExitStack
import concourse.bass as bass
import concourse.mybir as mybir
from concourse.bass_utils import run_bass_kernel_spmd

F32 = mybir.dt.float32
BF16 = mybir.dt.bfloat16
AF = mybir.ActivationFunctionType
ALU = mybir.AluOpType
AX = mybir.AxisListType

D = 1024
SEQ = 4096
NT = 32
PAST = 16384
GROUPS = ((128, 1), (512, 4), (2048, 16))
RING = (2, 5, 17)
RINGV = (3, 6, 18)
OFF = dict(qa=0, ka=768, va=1536, qb=2304, kb=2816, vb=3328, rb=4352, glr=5376, ga=5392, gb=6416)
INW = 7440
EPS = 1e-6
NSLAB = 6
DBG = dict(nprompt=8, sample=True, stop=None, ncores=8, setup=7)


class Res:
    __slots__ = ("name", "w", "r", "sem", "cnt", "const", "psum")

    def __init__(self, name, const=False, psum=False):
        self.psum = psum
        self.name = name
        self.w = None
        self.r = {}
        self.sem = None
        self.cnt = 0
        self.const = const


PE_LOG = []
OPLOG = {}
LAST_K = None


class PEProxy:
    def __init__(self, pe, kb):
        self._pe = pe
        self._kb = kb

    def matmul(self, *a, **kw):
        PE_LOG.append(self._kb.tag)
        return self._pe.matmul(*a, **kw)

    def __getattr__(self, n):
        return getattr(self._pe, n)


class KB:
    def __init__(self, nc, es):
        self.nc = nc
        self.es = es
        self.engs = {"pe": PEProxy(nc.tensor, self), "act": nc.scalar, "dve": nc.vector, "pool": nc.gpsimd, "sp": nc.sync}
        self.tag = "setup"
        self.sems = {}
        self.cnt = {}
        self.seen = {e: {} for e in self.engs}
        self.epoch = 0
        self.nsem = 0
        self.store_keys = set()

    def _sem(self, key):
        if key not in self.sems:
            self.nsem += 1
            self.sems[key] = self.es.enter_context(self.nc.semaphore(f"s{self.nsem}"))
            self.cnt[key] = 0
        return self.sems[key]

    def _wait(self, eng, key, val):
        if self.seen[eng].get(key, 0) >= val:
            return
        self.engs[eng].wait_ge(self.sems[key], val)
        self.seen[eng][key] = val

    def _deps(self, eng, reads, writes):
        deps = {}

        def add(tok):
            if tok is None:
                return
            k, v = tok
            if deps.get(k, 0) < v:
                deps[k] = v

        for r in reads:
            add(r.w)
            if r.psum:
                for k, v in r.r.items():
                    if not (k[0] == "e" and k[1] == eng):
                        add((k, v))
        for w in writes:
            add(w.w)
            for k, v in w.r.items():
                add((k, v))
        for k, v in deps.items():
            if eng == "pe" and k[0] == "e" and k[1] == "pe":
                continue
            self._wait(eng, k, v)

    def _commit(self, tok, reads, writes):
        k, v = tok
        for w in writes:
            w.w = tok
            w.r = {}
        for r in reads:
            if r.const:
                continue
            if r.r.get(k, 0) < v:
                r.r[k] = v

    def op(self, eng, fn, reads=(), writes=()):
        self._deps(eng, reads, writes)
        ins = fn(self.engs[eng])
        key = ("e", eng, self.epoch)
        sem = self._sem(key)
        self.cnt[key] += 1
        ins.then_inc(sem, 1)
        tok = (key, self.cnt[key])
        if eng == "pe":
            self.seen[eng][key] = self.cnt[key]
        if DBG.get("oplog"):
            import sys
            OPLOG[tok] = (self.tag, sys._getframe(1).f_lineno)
        self._commit(tok, reads, writes)

    def dma(self, eng, out, in_, semres, reads=(), writes=(), store=False, wnodep=()):
        self._deps(eng, reads, writes)
        writes = list(writes) + list(wnodep)
        ins = self.engs[eng].dma_start(out=out, in_=in_)
        key = ("d", semres.name)
        sem = self._sem(key)
        self.cnt[key] += 16
        ins.then_inc(sem, 16)
        tok = (key, self.cnt[key])
        if store:
            self.store_keys.add(key)
        self._commit(tok, reads, writes)

    def barrier(self):
        last = {}
        for key, c in self.cnt.items():
            if key[0] == "e" and c > 0:
                e = key[1]
                if e not in last or key[2] > last[e][0][2]:
                    last[e] = (key, c)
        for eng in self.engs:
            for e2, (key, c) in last.items():
                if e2 != eng:
                    self._wait(eng, key, c)

    def finish(self):
        for key in sorted(self.store_keys):
            self._wait("sp", key, self.cnt[key])


class Buf:
    def __init__(self, k, name, shape, dt, nres=1, const=False, psum=False, es=None):
        es = k.es if es is None else es
        if psum:
            self.t = es.enter_context(k.nc.psum_tensor(name, shape, dt))
        else:
            self.t = es.enter_context(k.nc.sbuf_tensor(name, shape, dt))
        self.res = [Res(f"{name}_{i}", const, psum) for i in range(nres)]

    @property
    def r(self):
        return self.res[0]


class Rot:
    def __init__(self, items):
        self.items = items
        self.i = 0

    def next(self):
        x = self.items[self.i % len(self.items)]
        self.i += 1
        return x


def build():
    nc = bass.Bass("TRN2", target_bir_lowering=False)

    def din(name, shape, dt=F32):
        return nc.dram_tensor(name, list(shape), dt, kind="ExternalInput").ap()

    def dout(name, shape):
        return nc.dram_tensor(name, list(shape), F32, kind="ExternalOutput").ap()

    def dint(name, shape, dt):
        return nc.dram_tensor(name, list(shape), dt, kind="Internal").ap()

    xp = din("xp", [SEQ, D])
    xs = din("xs", [4, D])
    cache = [(din(f"c{g}k", [4, GROUPS[g][0], 256]), din(f"c{g}v", [4, GROUPS[g][0], 256])) for g in range(3)]
    sg = din("sg", [4, 4, 128, 256])
    g1_d = din("g1", [128, 8])
    g2_d = din("g2", [128, 8])
    gf_d = din("gf", [128, D])
    gg_d = din("gg", [128, 256])
    w_in = din("w_in", [D, INW])
    wgk_d = din("wgk", [17, 512])
    w_pa = din("w_pa", [256, D])
    w_pb = din("w_pb", [D, D])
    w_o = din("w_o", [D, D])
    w_up = din("w_up", [D, 4 * D])
    w_dn = din("w_dn", [4 * D, D])
    c_cos = din("c_cos", [128, 33, 32])
    c_sin = din("c_sin", [128, 33, 32])
    c_mask = din("c_mask", [128, 9, 128])
    c_id = din("c_id", [128, 128])
    c_tri = din("c_tri", [128, 2, 128])
    c_tri1 = din("c_tri1", [128, 128])
    c_E = din("c_E", [128, 12, 128])

    yp = dout("yp", [SEQ, D])
    ys = dout("ys", [4, D])
    pkv = [(dout(f"p{g}k", [GROUPS[g][0], 256]), dout(f"p{g}v", [GROUPS[g][0], 256])) for g in range(3)]
    pgla = dout("pgla", [4, 128, 256])
    skv = [(dout(f"s{g}k", [4, GROUPS[g][0], 256]), dout(f"s{g}v", [4, GROUPS[g][0], 256])) for g in range(3)]
    sgla = dout("sgla", [4, 4, 128, 256])


    es = ExitStack()
    with es:
        k = KB(nc, es)
        global LAST_K
        LAST_K = k

        xres = Buf(k, "xres", [128, 4, D], F32, nres=4)
        actTs = [Buf(k, f"actT{i}", [128, 8, 512], BF16, nres=4) for i in range(2)]
        actT = actTs[0]
        xst = Buf(k, "xst", [128, D], F32)
        oaT = Buf(k, "oaT", [128, 2, 512], BF16, nres=4)
        obT = Buf(k, "obT", [128, 8, 512], BF16, nres=4)
        mixT = Buf(k, "mixT", [128, 8, 512], BF16, nres=8)
        fT = mixT
        qbT = Buf(k, "qbT", [128, 4, 512], BF16, nres=4)
        kbT = Buf(k, "kbT", [128, 4, 512], BF16, nres=4)
        glrT = Buf(k, "glrT", [32, 512], BF16)
        slabs = [Buf(k, f"slab{i}", [128, 8, 512], BF16) for i in range(NSLAB)]
        Sst = Buf(k, "Sst", [128, 4, 256], F32, nres=4)
        ident = Buf(k, "ident", [128, 128], BF16, const=True)
        masks = Buf(k, "masks", [128, 9, 128], BF16, const=True)
        cosb = Buf(k, "cosb", [128, 4, 32], F32)
        sinb = Buf(k, "sinb", [128, 4, 32], F32)
        g1b = Buf(k, "g1b", [128, 8], F32, const=True)
        g2b = Buf(k, "g2b", [128, 8], F32, const=True)
        gfb = Buf(k, "gfb", [128, D], F32, const=True)
        ggb = Buf(k, "ggb", [128, 256], F32, const=True)
        wgk = Buf(k, "wgkb", [17, 512], BF16, const=True)
        trib = Buf(k, "trib", [128, 2, 128], F32, const=True)
        tri1 = Buf(k, "tri1", [128, 128], F32, const=True)
        onec = Buf(k, "onec", [128, 8], F32, const=True)
        junk = Buf(k, "junk", [128, D], BF16)
        ss = Rot([Buf(k, f"ss{i}", [128, 8], F32) for i in range(3)])
        ubf = Rot([Buf(k, f"ubf{i}", [128, D], BF16) for i in range(2)])
        ropet = [Buf(k, f"ropet{i}", [128, 8, 32], F32) for i in range(4)]
        qkr = Rot([Buf(k, f"qkr{i}", [128, 512], F32) for i in range(2)])
        vf = Rot([Buf(k, f"vf{i}", [128, 256], F32) for i in range(2)])
        rec = Rot([Buf(k, f"rec{i}", [128, 4], F32) for i in range(2)])
        oab = Rot([Buf(k, f"oab{i}", [128, 256], BF16) for i in range(2)])
        esb = Buf(k, "esb", [128, 512], F32)
        spb = esb
        e1 = Buf(k, "e1", [128, 4, 128], F32)
        nlast = Buf(k, "nlast", [128, 4], F32)
        vbs = Buf(k, "vbs", [128, D], BF16)
        vbs2 = [vbs, Buf(k, "vbsB", [128, D], BF16)]
        dec2 = [Buf(k, f"dec{i}", [128, 4], F32) for i in range(2)]
        gsb = Buf(k, "gsb", [128, D], BF16)
        obb = Buf(k, "obb", [128, D], BF16)
        sga = Rot([Buf(k, f"sga{i}", [128, 512], BF16) for i in range(2)])
        tt = Rot([Buf(k, f"tt{i}", [128, 512], BF16) for i in range(2)])
        banks = [Buf(k, f"pb{i}", [128, 512], F32, psum=True) for i in range(8)]
        mmP4 = Rot(banks[0:4])
        mmP6 = Rot(banks[0:6])
        mmP5 = Rot(banks[0:5])
        accP1 = Rot(banks[5:6])
        accP2 = Rot(banks[4:6])
        mmP = mmP4
        accP = Rot(banks[4:6])
        trP2 = Rot(banks[6:8])
        trP3 = Rot(banks[5:8])
        trP = trP2
        es_p = ExitStack()
        kTr = [Buf(k, f"kTr{g}", [128, 2, RING[g] * 128], BF16, nres=RING[g], es=es_p) for g in range(3)]
        vAr = [Buf(k, f"vAr{g}", [128, RINGV[g], 4, 65], BF16, nres=RINGV[g], es=es_p) for g in range(3)]
        qT = Buf(k, "qT", [128, 12, 128], BF16, nres=3, es=es_p)
        Sbf = Buf(k, "Sbf", [128, 4, 256], BF16, es=es_p)
        qkb = Rot([Buf(k, f"qkb{i}", [128, 512], BF16, es=es_p) for i in range(3)])
        pTs = Rot([Buf(k, f"pT{i}", [128, 512], BF16, es=es_p) for i in range(4)])
        e2 = Buf(k, "e2", [128, 4, 128], BF16, es=es_p)
        e3 = Buf(k, "e3", [128, 4, 128], BF16, es=es_p)
        qd = Buf(k, "qd", [128, 4, 128], BF16, es=es_p)
        qd2 = [qd, Buf(k, "qdB", [128, 4, 128], BF16, es=es_p)]
        kd = Buf(k, "kd", [128, 4, 128], BF16, es=es_p)
        klT = Buf(k, "klT", [128, 4, 128], BF16, es=es_p)
        kls = Buf(k, "kls", [128, 4, 128], BF16, es=es_p)
        kls2 = [kls, Buf(k, "klsB", [128, 4, 128], BF16, es=es_p)]
        ats = Buf(k, "ats", [128, 4, 128], BF16, es=es_p)
        ats2 = [ats, Buf(k, "atsB", [128, 4, 128], BF16, es=es_p)]

        r_win, r_wpa, r_wpb, r_wo, r_wup, r_wdn = (Res(n) for n in ("win", "wpa", "wpb", "wo", "wup", "wdn"))
        r_out = Res("outs")

        def cast_dram(dst, src, res, rows, cols, clo=0):
            rs = 128
            cs = 2048
            if not DBG["setup"] & 1:
                return
            for r0 in range(0, rows, rs):
                for c0 in range(clo, cols, cs):
                    c1 = min(cols, c0 + cs)
                    k.dma("pool", dst[r0:r0 + rs, c0:c1], src[r0:r0 + rs, c0:c1], res, wnodep=[res])

        r_const = Res("consts")
        r_const2 = Res("consts_sw")
        cbufs = []

        def ld(buf, src, eng="sp"):
            if not DBG["setup"] & 4:
                return
            rc_ = r_const if eng == "sp" else r_const2
            k.dma(eng, buf.t[:], src, rc_, wnodep=[rc_])
            cbufs.append((buf, rc_))

        ld(ident, c_id, "pool")
        ld(masks, c_mask, "pool")
        ld(wgk, wgk_d, "pool")
        ld(g1b, g1_d)
        ld(g2b, g2_d)
        ld(gfb, gf_d)
        ld(ggb, gg_d)
        ld(trib, c_tri)
        ld(tri1, c_tri1)
        for b_, rc_ in cbufs:
            b_.r.w = rc_.w

        if DBG["setup"] & 8:
            k.finish()
            es_p.close()
            return nc
        r_cp = Res("cachecp")

        cc_list = []
        for g in range(3):
            W = GROUPS[g][0]
            for kv in range(2):
                for b in range(4):
                    cc_list.append((skv[g][kv][b, 0:W - 16, :], cache[g][kv][b, 1:W - 15, :]))
                    cc_list.append((skv[g][kv][b, W - 16:W - 1, :], cache[g][kv][b, W - 15:W, :]))
        cc_big = [c for c in cc_list if c[0].shape[0] > 1000]
        cc_small = [c for c in cc_list if c[0].shape[0] <= 1000]

        def cache_copies(blk):
            todo = [cc_big[blk]] + cc_small[blk * 5:(blk + 1) * 5]
            for (o_, i_) in todo:
                k.dma("sp", o_, i_, r_cp, store=True)
        k.op("pool", lambda e: e.memset(onec.t[:], 1.0), writes=[onec.r])
        k.op("pool", lambda e: e.memset(glrT.t[:], 1.0), writes=[glrT.r])
        for g in range(3):
            k.op("pool", lambda e, g=g: e.memset(vAr[g].t[:], 1.0), writes=vAr[g].res)
        k.op("pool", lambda e: e.memset(Sst.t[:], 0.0), writes=Sst.res)
        k.op("pool", lambda e: e.memset(Sbf.t[:], 0.0), writes=[Sbf.r])
        k.op("pool", lambda e: e.memset(qT.t[:], 0.0), writes=qT.res)

        WSRC = dict(win=w_in, wpa=w_pa, wpb=w_pb, wo=w_o, wup=w_up, wdn=w_dn)

        def mk_uses():
            u = []
            for g in range(3):
                u.append([("win", 0, 8, OFF["qa"] + 256 * g, 256, 0), ("win", 0, 8, OFF["ka"] + 256 * g, 256, 256)])
            u.append([("win", 0, 8, OFF["va"], 512, 0)])
            u.append([("win", 0, 8, OFF["va"] + 512, 256, 0)])
            u.append([("win", 0, 8, OFF["qb"], 512, 0)])
            u.append([("win", 0, 8, OFF["kb"], 512, 0)])
            u.append([("win", 0, 8, OFF["glr"], 16, 0)])
            u.append([("win", 0, 8, OFF["vb"], 512, 0)])
            u.append([("win", 0, 8, OFF["vb"] + 512, 512, 0)])
            u.append([("win", 0, 8, OFF["rb"], 512, 0)])
            u.append([("win", 0, 8, OFF["rb"] + 512, 512, 0)])
            for s_ in range(2):
                u.append([("win", 0, 8, OFF["ga"] + 512 * s_, 512, 0)])
                u.append([("win", 0, 8, OFF["gb"] + 512 * s_, 512, 0)])
                u.append([("wpa", 0, 2, 512 * s_, 512, 0)])
                u.append([("wpb", 0, 8, 512 * s_, 512, 0)])
            for s_ in range(2):
                u.append([("wo", 0, 8, 512 * s_, 512, 0)])
            for q in range(4):
                u.append([("wup", 0, 8, 1024 * q, 512, 0)])
                u.append([("wup", 0, 8, 1024 * q + 512, 512, 0)])
                u.append([("wdn", 1024 * q, 8, 0, 512, 0)])
                u.append([("wdn", 1024 * q, 8, 512, 512, 0)])
            return u

        utypes = mk_uses()
        NU = len(utypes)
        wsl = dint("wsl", [NU, 128, 8 * 512], BF16)
        cgroups = [(0, 1), (1, 3), (3, 5), (5, 12), (12, 24), (24, NU)]
        cres = [Res(f"cast{i}") for i in range(len(cgroups))]
        ures = [None] * NU
        for gi, (a_, b_) in enumerate(cgroups):
            for v in range(a_, b_):
                ures[v] = cres[gi]
                for (wk, r0, nkc, c0, w, off) in utypes[v]:
                    if DBG["setup"] & 1:
                        k.dma("pool", wsl[v].rearrange("p (k n) -> p k n", k=8)[:, 0:nkc, off:off + w],
                              WSRC[wk][r0:r0 + nkc * 128, c0:c0 + w].rearrange("(k p) n -> p k n", p=128),
                              cres[gi], wnodep=[cres[gi]])

        NBLK = 9
        uses = []
        for _ in range(NBLK):
            uses += list(range(NU))

        class Slabs:
            def __init__(self):
                self.loaded = 0
                self.nxt = 0
                self.released = set()

            def pump(self):
                while self.loaded < len(uses) and (self.loaded < NSLAB or (self.loaded - NSLAB) in self.released):
                    v = self.loaded
                    b = slabs[v % NSLAB]
                    ut = uses[v]
                    nkc = utypes[ut][0][2]
                    wtot = max(off + w for (_, _, _, _, w, off) in utypes[ut])
                    if wtot == 512:
                        k.dma("sp", b.t[:, 0:nkc, :].rearrange("p k n -> p (k n)"), wsl[ut][:, 0:nkc * 512],
                              b.r, reads=[ures[ut]], writes=[b.r])
                    else:
                        k.dma("sp", b.t[:, 0:nkc, 0:wtot], wsl[ut].rearrange("p (k n) -> p k n", k=8)[:, 0:nkc, 0:wtot],
                              b.r, reads=[ures[ut]], writes=[b.r])
                    self.loaded += 1

            def get(self):
                u = self.nxt
                self.nxt += 1
                self.pump()
                assert self.loaded > u, "slab ring deadlock"
                return u, slabs[u % NSLAB]

            def rel(self, u):
                self.released.add(u)
                self.pump()

        SL = Slabs()

        cur = {"blk": 0}

        def PL():
            return "dve" if cur["blk"] == 0 else "pool"

        def mm_acc(ps_ap, pairs, reads, psres):
            def fn(e):
                ins = None
                n = len(pairs)
                for i, (l, r) in enumerate(pairs):
                    ins = e.matmul(ps_ap, l, r, start=(i == 0), stop=(i == n - 1))
                return ins
            k.op("pe", fn, reads=reads, writes=[psres])

        def transposes(src_buf, src_ap_fn, n, dst_fn, dst_res=None):
            for c0 in range(0, n, 4):
                m = min(4, n - c0)
                pb = trP.next()

                def fn(e, c0=c0, m=m, pb=pb):
                    ins = None
                    for lc in range(m):
                        ins = e.matmul(pb.t[:, lc * 128:(lc + 1) * 128], src_ap_fn(c0 + lc), ident.t[:, :], start=True, stop=True)
                    return ins
                k.op("pe", fn, reads=[src_buf.r, ident.r], writes=[pb.r])
                dst_fn(pb.t[:, :], pb, c0)

        def cp3(eng, out3, in3, pb, dst_res):
            if eng == "act":
                k.op("act", lambda e: e.activation(out=out3, in_=in3, func=AF.Copy), reads=[pb.r], writes=[dst_res])
            else:
                k.op("dve", lambda e: e.tensor_copy(out=out3, in_=in3), reads=[pb.r], writes=[dst_res])

        def norm_tile(x_ap, xr, gb, out_ap, outr, feat_gain=False):
            s = ss.next()
            k.op("act", lambda e: e.activation(out=junk.t[:, :], in_=x_ap, func=AF.Square, scale=1.0 / 32.0,
                                               accum_out=s.t[:, 0:1]), reads=[xr], writes=[junk.r, s.r])
            k.op("act", lambda e: e.activation(out=s.t[:, 4:5], in_=s.t[:, 0:1], func=AF.Ln, bias=EPS), reads=[s.r], writes=[s.r])
            k.op("act", lambda e: e.activation(out=s.t[:, 1:2], in_=s.t[:, 4:5], func=AF.Exp, scale=-0.5), reads=[s.r], writes=[s.r])
            if feat_gain:
                k.op("dve", lambda e: e.tensor_scalar(out=out_ap, in0=x_ap, scalar1=s.t[:, 1:2], scalar2=None, op0=ALU.mult),
                     reads=[xr, s.r], writes=[outr])
            else:
                assert gb is not None
                k.op("dve", lambda e: e.scalar_tensor_tensor(out=out_ap, in0=x_ap, scalar=s.t[:, 1:2], in1=gb.t[:, :],
                                                             op0=ALU.mult, op1=ALU.mult), reads=[xr, s.r, gb.r], writes=[outr])

        def norm_phase(x_ap, xr):
            u = ubf.next()
            norm_tile(x_ap, xr, None, u.t[:, :], u.r, feat_gain=True)
            return u

        def xpose_phase(u, j, gb, dbuf):
            def dst(pv, pb, c0):
                for lc in range(4):
                    c = c0 + lc
                    src = pv[:, lc * 128:(lc + 1) * 128]
                    if lc % 2 == 0:
                        k.op("act", lambda e: e.mul(out=dbuf.t[:, c, j * 128:(j + 1) * 128], in_=src, mul=gb.t[:, c:c + 1]),
                             reads=[pb.r, gb.r], writes=[dbuf.res[j]])
                    else:
                        k.op("dve", lambda e: e.tensor_scalar(out=dbuf.t[:, c, j * 128:(j + 1) * 128], in0=src,
                                                              scalar1=gb.t[:, c:c + 1], scalar2=None, op0=ALU.mult),
                             reads=[pb.r, gb.r], writes=[dbuf.res[j]])
            transposes(u, lambda c: u.t[:, c * 128:(c + 1) * 128], 8, dst, None)

        def to_actT(j, x_ap, xr, gb):
            u = norm_phase(x_ap, xr)
            xpose_phase(u, j, gb, actT)

        def proj_tok(j, slab, width, nkc=8, src=None, srcres=None):
            src = actT if src is None else src
            pb = mmP.next()
            pairs = [(src.t[:, kc, j * 128:(j + 1) * 128], slab.t[:, kc, 0:width]) for kc in range(nkc)]
            mm_acc(pb.t[:, 0:width], pairs, [src.res[j], slab.r], pb.r)
            return pb

        def proj_feat(slab, lc, N, ntile, nkc=8, src=None, rows=128):
            src = actT if src is None else src
            pb = mmP.next()
            pairs = [(slab.t[:, kc, lc * 128:lc * 128 + rows], src.t[:, kc, 0:N]) for kc in range(nkc)]
            mm_acc(pb.t[0:rows, 0:N], pairs, [src.res[j] for j in range(ntile)] + [slab.r], pb.r)
            return pb

        def run_block(blk):
            sample = blk == 8
            cur["blk"] = blk
            ntile = 1 if sample else 4
            N = ntile * 128
            T0 = blk * 4

            k.tag = "s0"
            nonlocal actT
            actT = actTs[blk % 2]
            ctile = 32 if sample else T0
            k.dma("sp", cosb.t[:, 0:ntile, :], c_cos[:, ctile:ctile + ntile, :], cosb.r, writes=[cosb.r])
            k.dma("sp", sinb.t[:, 0:ntile, :], c_sin[:, ctile:ctile + ntile, :], sinb.r, writes=[sinb.r])
            for j in range(ntile):
                if sample:
                    k.op("pool", lambda e: e.memset(xres.t[:, 0, :], 0.0), writes=[xres.res[0]])
                    k.dma("sp", xres.t[0:4, 0, :], xs, xres.res[0], writes=[xres.res[0]])
                else:
                    t0 = (T0 + j) * 128
                    k.dma("sp", xres.t[:, j, :], xp[t0:t0 + 128, :], xres.res[j], writes=[xres.res[j]])
                if blk == 0:
                    to_actT(j, xres.t[:, j, :], xres.res[j], g1b)

            if DBG["stop"] == "s0":
                return
            nonlocal mmP, accP, trP
            trP = trP2
            if not sample:
                mmP, accP = mmP5, accP1
            uqk = [SL.get() for _ in range(3)]
            uv = [SL.get() for _ in range(2)]

            def produce(j, g):
                k.tag = "A.prod"
                T = 32 if sample else T0 + j
                W, dil = GROUPS[g]
                slot = T % RING[g]
                vslot = T % RINGV[g]
                pb = proj_tok(j, uqk[g][1], 512)
                x4 = pb.t[:, 0:512].rearrange("p (h two i) -> p h two i", two=2, i=32)
                x1, x2 = x4[:, :, 0, :], x4[:, :, 1, :]
                cb = cosb.t[:, j, :].unsqueeze(1).to_broadcast([128, 8, 32])
                sb = sinb.t[:, j, :].unsqueeze(1).to_broadcast([128, 8, 32])
                if sample:
                    qr_ap, qr_res = sqk.t[:, g, :], sqk.res[g]
                else:
                    qr = qkr.next()
                    qr_ap, qr_res = qr.t[:, :], qr.r
                o4 = qr_ap.rearrange("p (h two i) -> p h two i", two=2, i=32)
                t1, t2, t3, t4 = ropet
                k.op("dve", lambda e: e.tensor_tensor(out=t1.t[:], in0=x1, in1=cb, op=ALU.mult), reads=[pb.r, cosb.r], writes=[t1.r])
                k.op("dve", lambda e: e.tensor_tensor(out=t2.t[:], in0=x2, in1=sb, op=ALU.mult), reads=[pb.r, sinb.r], writes=[t2.r])
                k.op("dve", lambda e: e.tensor_tensor(out=o4[:, :, 0, :], in0=t1.t[:], in1=t2.t[:], op=ALU.subtract),
                     reads=[t1.r, t2.r], writes=[qr_res])
                k.op("dve", lambda e: e.tensor_tensor(out=t3.t[:], in0=x2, in1=cb, op=ALU.mult), reads=[pb.r, cosb.r], writes=[t3.r])
                k.op("dve", lambda e: e.tensor_tensor(out=t4.t[:], in0=x1, in1=sb, op=ALU.mult), reads=[pb.r, sinb.r], writes=[t4.r])
                k.op(PL(), lambda e: e.tensor_tensor(out=o4[:, :, 1, :], in0=t3.t[:], in1=t4.t[:], op=ALU.add),
                     reads=[t3.r, t4.r], writes=[qr_res])
                vsl = uv[0][1] if g < 2 else uv[1][1]
                vc0 = (g % 2) * 256 if g < 2 else 0
                pv_ = mmP.next()
                pairs = [(actT.t[:, kc, j * 128:(j + 1) * 128], vsl.t[:, kc, vc0:vc0 + 256]) for kc in range(8)]
                mm_acc(pv_.t[:, 0:256], pairs, [actT.res[j], vsl.r], pv_.r)
                if sample:
                    k.op("act", lambda e: e.activation(out=svv.t[:, g, :], in_=pv_.t[:, 0:256], func=AF.Copy),
                         reads=[pv_.r], writes=[svv.res[g]])
                    return None
                tout = T - (NT - W // 128)
                if tout >= 0:
                    k.dma("sp", pkv[g][0][tout * 128:(tout + 1) * 128, :], qr_ap[:, 256:512], qr_res,
                          reads=[qr_res], store=True)
                    vfb = vf.next()
                    k.op("act", lambda e: e.activation(out=vfb.t[:, 0:256], in_=pv_.t[:, 0:256], func=AF.Copy),
                         reads=[pv_.r], writes=[vfb.r])
                    k.dma("sp", pkv[g][1][tout * 128:(tout + 1) * 128, :], vfb.t[:, 0:256], vfb.r,
                          reads=[vfb.r], store=True)
                k.op("act", lambda e: e.activation(out=vAr[g].t[:, vslot, :, 0:64],
                                                   in_=pv_.t[:, 0:256].rearrange("p (h d) -> p h d", d=64), func=AF.Copy),
                     reads=[pv_.r], writes=[vAr[g].res[vslot]])
                qb_ = qkb.next()
                if cur["blk"] == 0:
                    k.op("act", lambda e: e.activation(out=qb_.t[:, :], in_=qr_ap, func=AF.Copy), reads=[qr_res], writes=[qb_.r])
                else:
                    k.op("pool", lambda e: e.tensor_copy(out=qb_.t[:, :], in_=qr_ap), reads=[qr_res], writes=[qb_.r])
                return (j, g, slot, qb_)

            def xpose(ctx):
                k.tag = "A.xpose"
                j, g, slot, qb_ = ctx

                def dst(pv, pb2, c0):
                    for hh in range(2):
                        cp3("act", qT.t[hh * 64:(hh + 1) * 64, 4 * g + hh:4 * g + 4:2, :],
                            pv[hh * 64:(hh + 1) * 64, 0:256].rearrange("p (c t) -> p c t", c=2), pb2, qT.res[g])
                    cp3("act", kTr[g].t[:, :, slot * 128:(slot + 1) * 128], pv[:, 256:512].rearrange("p (c t) -> p c t", c=2),
                        pb2, kTr[g].res[slot])
                transposes(qb_, lambda c: qb_.t[:, c * 128:(c + 1) * 128], 4, dst)

            def attention(j):
                k.tag = "A.attn"
                T = T0 + j
                ab = accP.next()
                units = []
                for g in range(3):
                    W, dil = GROUPS[g]
                    nd = W // 128
                    for dlt in range(0, min(T, nd) + 1):
                        kind = 0 if dlt == 0 else (2 if dlt == nd else 1)
                        units.append((g, (T - dlt) % RING[g], kind, (T - dlt) % RINGV[g]))
                nu = len(units)

                def emit_qk(ui):
                    g, slot, kind, vslot = units[ui]
                    sp_ = mmP.next()

                    def fqk(e):
                        ins = None
                        for c in range(2):
                            ins = e.matmul(sp_.t[:, c * 256:(c + 1) * 256],
                                           kTr[g].t[:, c, slot * 128:(slot + 1) * 128],
                                           qT.t[:, 4 * g + 2 * c:4 * g + 2 * c + 2, :].rearrange("p h t -> p (h t)"), start=True, stop=True)
                        return ins
                    k.op("pe", fqk, reads=[kTr[g].res[slot], qT.res[g]], writes=[sp_.r])
                    return sp_
                sps = {}
                for ui in range(min(3, nu)):
                    sps[ui] = emit_qk(ui)
                for ui in range(nu):
                    g, slot, kind, vslot = units[ui]
                    sp_ = sps.pop(ui)
                    pt = pTs.next()
                    k.op("act", lambda e: e.activation(out=pt.t[:, :], in_=sp_.t[:, :], func=AF.Exp, scale=0.125),
                         reads=[sp_.r], writes=[pt.r])
                    p3 = pt.t[:, :].rearrange("p (h t) -> p h t", h=4)
                    mk = masks.t[:, 3 * g + kind, :].unsqueeze(1).to_broadcast([128, 4, 128])
                    k.op("dve", lambda e: e.tensor_tensor(out=p3, in0=p3, in1=mk, op=ALU.mult),
                         reads=[pt.r, masks.r], writes=[pt.r])
                    if ui + 3 < nu:
                        sps[ui + 3] = emit_qk(ui + 3)

                    def fpv(e):
                        ins = None
                        for h in range(4):
                            ins = e.matmul(ab.t[:, h * 65:(h + 1) * 65], pt.t[:, h * 128:(h + 1) * 128],
                                           vAr[g].t[:, vslot, h, :], start=(ui == 0 and h == 0), stop=(ui == nu - 1),
                                           skip_group_check=True)
                        return ins
                    k.op("pe", fpv, reads=[pt.r, vAr[g].res[vslot]], writes=[ab.r])
                a3 = ab.t[:, 0:260].rearrange("p (h e) -> p h e", e=65)
                rc = rec.next()
                k.op("dve", lambda e: e.reciprocal(out=rc.t[:, :].unsqueeze(2), in_=a3[:, :, 64:65]), reads=[ab.r], writes=[rc.r])
                ob_ = oab.next()
                k.op("dve", lambda e: e.tensor_tensor(out=ob_.t[:, :].rearrange("p (h d) -> p h d", d=64), in0=a3[:, :, 0:64],
                                                      in1=rc.t[:, :].unsqueeze(2).to_broadcast([128, 4, 64]), op=ALU.mult),
                     reads=[ab.r, rc.r], writes=[ob_.r])

                def attn_late():
                    k.tag = "A.attn"

                    def dst(pv, pb2, c0):
                        cp3("act", oaT.t[:, :, j * 128:(j + 1) * 128], pv[:, 0:256].rearrange("p (c t) -> p c t", c=2), pb2, oaT.res[j])
                    transposes(ob_, lambda c: ob_.t[:, c * 128:(c + 1) * 128], 2, dst)
                lates.append(attn_late)

            lates = []
            pending = []
            for j in range(ntile):
                for g in range(3):
                    ctx = produce(j, g)
                    while lates:
                        lates.pop(0)()
                    if ctx is not None:
                        pending.append(ctx)
                    if len(pending) > 2:
                        c_ = pending.pop(0)
                        xpose(c_)
                        if c_[1] == 2:
                            attention(c_[0])
            while pending:
                c_ = pending.pop(0)
                xpose(c_)
                if c_[1] == 2:
                    attention(c_[0])
            while lates:
                lates.pop(0)()
            if sample:
                sample_attention()
            for u_, _ in uqk + uv:
                SL.rel(u_)

            if DBG["stop"] == "A":
                return
            k.tag = "B"
            accP = accP2
            mmP = mmP4 if sample else mmP5
            trP = trP2 if sample else trP3
            for wi, (dstb, scale) in enumerate(((qbT, 128.0 ** -0.5), (kbT, 1.0))):
                u_, sl = SL.get()
                for h in range(4):
                    pb = proj_feat(sl, h, N, ntile)
                    k.op("act", lambda e: e.mul(out=dstb.t[:, h, 0:N], in_=pb.t[:, 0:N], mul=scale),
                         reads=[pb.r], writes=[dstb.res[h]])
                SL.rel(u_)
            u_, sl = SL.get()
            pb = proj_feat(sl, 0, N, ntile, rows=16)
            k.op("act", lambda e: e.activation(out=glrT.t[0:16, 0:N], in_=pb.t[0:16, 0:N], func=AF.Copy), reads=[pb.r], writes=[glrT.r])
            SL.rel(u_)
            uvb = [SL.get() for _ in range(2)]
            urb = [SL.get() for _ in range(2)]

            def prep_a(j):
                P = j % 2
                cs = slice(j * 128, (j + 1) * 128)
                qd_, ats_, kls_, vbs_, dec_ = qd2[P], ats2[P], kls2[P], vbs2[P], dec2[P]
                pb = mmP.next()
                mm_acc(pb.t[:, 0:512], [(glrT.t[0:17, cs], wgk.t[0:17, :])], [glrT.r, wgk.r], pb.r)
                k.op("act", lambda e: e.activation(out=esb.t[:, :], in_=pb.t[:, :], func=AF.Exp, scale=-1.0), reads=[pb.r], writes=[esb.r])
                k.op("act", lambda e: e.activation(out=spb.t[:, :], in_=esb.t[:, :], func=AF.Ln, bias=1.0), reads=[esb.r], writes=[spb.r])
                for half in range(2):
                    pv_ = proj_tok(j, uvb[half][1], 512)
                    k.op("act", lambda e: e.activation(out=vbs_.t[:, half * 512:(half + 1) * 512], in_=pv_.t[:, :], func=AF.Copy),
                         reads=[pv_.r], writes=[vbs_.r])
                    if sample:
                        k.op("dve", lambda e: e.tensor_copy(out=vbf.t[:, half * 512:(half + 1) * 512], in_=pv_.t[:, :]),
                             reads=[pv_.r], writes=[vbf.r])
                pc = mmP.next()

                def fcum(e):
                    ins = None
                    for h in range(4):
                        ins = e.matmul(pc.t[:, h * 128:(h + 1) * 128], spb.t[:, h * 128:(h + 1) * 128],
                                       trib.t[:, 1 if sample else 0, :], start=True, stop=True)
                    return ins
                k.op("pe", fcum, reads=[spb.r, trib.r], writes=[pc.r])
                pc3 = pc.t[:, :].rearrange("p (h t) -> p h t", h=4)
                k.op("act", lambda e: e.activation(out=e1.t[:], in_=pc3, func=AF.Exp, scale=-1.0), reads=[pc.r], writes=[e1.r])
                if not sample:
                    k.op("act", lambda e: e.activation(out=e2.t[:], in_=pc3, func=AF.Exp, scale=1.0), reads=[pc.r], writes=[e2.r])
                    k.op("dve", lambda e: e.tensor_copy(out=dec_.t[:, :].unsqueeze(2), in_=e1.t[:, :, 127:128]), reads=[e1.r], writes=[dec_.r])
                if sample:
                    return
                k.op("dve", lambda e: e.tensor_tensor(out=qd_.t[:], in0=e1.t[:], in1=qbT.t[:, :, cs], op=ALU.mult),
                     reads=[e1.r] + qbT.res, writes=[qd_.r])
                k.op("dve", lambda e: e.tensor_tensor(out=kd.t[:], in0=e2.t[:], in1=kbT.t[:, :, cs], op=ALU.mult),
                     reads=[e2.r] + kbT.res, writes=[kd.r])
                for h in range(4):
                    k.op("dve",
                         lambda e: e.scalar_tensor_tensor(out=klT.t[:, h, :], in0=e2.t[:, h, :], scalar=dec_.t[:, h:h + 1],
                                                          in1=kbT.t[:, h, cs], op0=ALU.mult, op1=ALU.mult),
                         reads=[e2.r, dec_.r, kbT.res[h]], writes=[klT.r])

            def prep_b(j):
                if sample:
                    return
                P = j % 2
                qd_, ats_, kls_, vbs_, dec_ = qd2[P], ats2[P], kls2[P], vbs2[P], dec2[P]

                def dst(pv, pb2, c0):
                    cp3("act", kls_.t[:], pv[:, 0:512].rearrange("p (h d) -> p h d", h=4), pb2, kls_.r)
                transposes(klT, lambda c: klT.t[:, c, :], 4, dst, None)
                pa_ = mmP.next()

                def fat(e):
                    ins = None
                    for h in range(4):
                        ins = e.matmul(pa_.t[:, h * 128:(h + 1) * 128], kd.t[:, h, :], qd_.t[:, h, :], start=True, stop=True)
                    return ins
                k.op("pe", fat, reads=[kd.r, qd_.r], writes=[pa_.r])
                k.op("dve", lambda e: e.tensor_tensor(out=ats_.t[:], in0=pa_.t[:, :].rearrange("p (h t) -> p h t", h=4),
                                                      in1=tri1.t[:, :].unsqueeze(1).to_broadcast([128, 4, 128]), op=ALU.mult),
                     reads=[pa_.r, tri1.r], writes=[ats_.r])

            def fin_a(j):
                for half in range(2):
                    pr_ = proj_tok(j, urb[half][1], 512)
                    k.op("act", lambda e: e.activation(out=gsb.t[:, half * 512:(half + 1) * 512], in_=pr_.t[:, :], func=AF.Silu),
                         reads=[pr_.r], writes=[gsb.r])
                g3 = gsb.t[:, :].rearrange("p (h v) -> p h v", h=4)
                k.op(PL(), lambda e: e.tensor_tensor(out=g3, in0=g3, in1=ggb.t[:, :].unsqueeze(1).to_broadcast([128, 4, 256]), op=ALU.mult),
                     reads=[gsb.r, ggb.r], writes=[gsb.r])

            def fin_b(j):
                P = j % 2
                T = T0 + j
                qd_, ats_, kls_, vbs_, dec_ = qd2[P], ats2[P], kls2[P], vbs2[P], dec2[P]
                if sample:
                    gla_sample()
                else:
                    for hb in range(2):
                        pbk = mmP.next()

                        def fo(e, hb=hb, pbk=pbk):
                            ins = None
                            for hh in range(2):
                                h = hb * 2 + hh
                                e.matmul(pbk.t[:, hh * 256:(hh + 1) * 256], ats_.t[:, h, :], vbs_.t[:, h * 256:(h + 1) * 256], start=(hh == 0), stop=False,
                                         skip_group_check=True)
                                ins = e.matmul(pbk.t[:, hh * 256:(hh + 1) * 256], qd_.t[:, h, :], Sbf.t[:, h, :], start=False, stop=True,
                                               skip_group_check=True)
                            return ins
                        k.op("pe", fo, reads=[ats_.r, vbs_.r, qd_.r, Sbf.r], writes=[pbk.r])
                        finish_o(hb, pbk)
                    for hb in range(2):
                        pbk = mmP.next()

                        def fs(e, hb=hb, pbk=pbk):
                            ins = None
                            for hh in range(2):
                                h = hb * 2 + hh
                                ins = e.matmul(pbk.t[:, hh * 256:(hh + 1) * 256], kls_.t[:, h, :], vbs_.t[:, h * 256:(h + 1) * 256], start=True, stop=True)
                            return ins
                        k.op("pe", fs, reads=[kls_.r, vbs_.r], writes=[pbk.r])
                        for hh in range(2):
                            h = hb * 2 + hh
                            k.op("dve", lambda e: e.scalar_tensor_tensor(out=Sst.t[:, h, :], in0=Sst.t[:, h, :], scalar=dec_.t[:, h:h + 1],
                                                                         in1=pbk.t[:, hh * 256:(hh + 1) * 256], op0=ALU.mult, op1=ALU.add),
                                 reads=[Sst.res[h], dec_.r, pbk.r], writes=[Sst.res[h]])
                            k.op("dve", lambda e: e.tensor_copy(out=Sbf.t[:, h, :], in_=Sst.t[:, h, :]),
                                 reads=[Sst.res[h]], writes=[Sbf.r])
                    if T == NT - 1:
                        k.dma("sp", pgla.rearrange("h d v -> d h v"), Sst.t[:], Sst.res[0], reads=Sst.res, store=True)

            def late(j):
                def dst(pv, pb2, c0, j=j):
                    cp3("act" if c0 == 0 else "dve", obT.t[:, c0:c0 + 4, j * 128:(j + 1) * 128],
                        pv[:, 0:512].rearrange("p (c t) -> p c t", c=4), pb2, obT.res[j])
                transposes(obb, lambda c: obb.t[:, c * 128:(c + 1) * 128], 8, dst, None)

            prep_a(0)
            prep_b(0)
            if ntile > 1:
                prep_a(1)
            fin_a(0)
            if ntile > 1:
                prep_b(1)
            for j in range(ntile):
                fin_b(j)
                if j + 2 < ntile:
                    prep_a(j + 2)
                if j + 1 < ntile:
                    fin_a(j + 1)
                late(j)
                if j + 2 < ntile:
                    prep_b(j + 2)
            for u_, _ in uvb + urb:
                SL.rel(u_)

            if DBG["stop"] == "B":
                return
            k.tag = "C"
            mmP = mmP4
            trP = trP2
            if not sample:
                cache_copies(blk)
            for s in range(2):
                uga, ugb, upa, upb = SL.get(), SL.get(), SL.get(), SL.get()
                for lc in range(4):
                    c = s * 4 + lc
                    pga = proj_feat(uga[1], lc, N, ntile)
                    sa = sga.next()
                    k.op("act", lambda e: e.activation(out=sa.t[:, 0:N], in_=pga.t[:, 0:N], func=AF.Sigmoid), reads=[pga.r], writes=[sa.r])
                    pgb = proj_feat(ugb[1], lc, N, ntile)
                    sb_ = sga.next()
                    k.op("act", lambda e: e.activation(out=sb_.t[:, 0:N], in_=pgb.t[:, 0:N], func=AF.Sigmoid), reads=[pgb.r], writes=[sb_.r])
                    ppa = proj_feat(upa[1], lc, N, ntile, nkc=2, src=oaT)
                    ta = tt.next()
                    k.op("dve", lambda e: e.tensor_tensor(out=ta.t[:, 0:N], in0=ppa.t[:, 0:N], in1=sa.t[:, 0:N], op=ALU.mult),
                         reads=[ppa.r, sa.r], writes=[ta.r])
                    ppb = proj_feat(upb[1], lc, N, ntile, src=obT)
                    tb = tt.next()
                    k.op("dve", lambda e: e.tensor_tensor(out=tb.t[:, 0:N], in0=ppb.t[:, 0:N], in1=sb_.t[:, 0:N], op=ALU.mult),
                         reads=[ppb.r, sb_.r], writes=[tb.r])
                    k.op(PL(), lambda e: e.tensor_tensor(out=mixT.t[:, c, 0:N], in0=ta.t[:, 0:N], in1=tb.t[:, 0:N], op=ALU.add),
                         reads=[ta.r, tb.r], writes=[mixT.res[c]])
                for u_, _ in (uga, ugb, upa, upb):
                    SL.rel(u_)

            if DBG["stop"] == "C":
                return
            k.tag = "D"
            uwo = [SL.get(), SL.get()]
            dnorm = []
            for j in range(ntile):
                for nh in range(2):
                    sl = uwo[nh][1]
                    pb = mmP.next()
                    pairs = [(mixT.t[:, kc, j * 128:(j + 1) * 128], sl.t[:, kc, :]) for kc in range(8)]
                    mm_acc(pb.t[:, :], pairs, mixT.res + [sl.r], pb.r)
                    xa = xres.t[:, j, nh * 512:(nh + 1) * 512]
                    k.op("dve", lambda e: e.tensor_tensor(out=xa, in0=xa, in1=pb.t[:, :], op=ALU.add), reads=[pb.r, xres.res[j]], writes=[xres.res[j]])
                if j > 0:
                    dnorm.append((j - 1, norm_phase(xres.t[:, j - 1, :], xres.res[j - 1])))
                if len(dnorm) > 1:
                    jj, uu_ = dnorm.pop(0)
                    xpose_phase(uu_, jj, g2b, actT)
            dnorm.append((ntile - 1, norm_phase(xres.t[:, ntile - 1, :], xres.res[ntile - 1])))
            while dnorm:
                jj, uu_ = dnorm.pop(0)
                xpose_phase(uu_, jj, g2b, actT)
            for u_, _ in uwo:
                SL.rel(u_)

            if DBG["stop"] == "D":
                return
            k.tag = "E"
            nxt_sample = (blk == 7)
            do_pf = blk < 8
            for q in range(4):
                pf_u = None
                if do_pf and (not nxt_sample or q == 0):
                    if nxt_sample:
                        k.op("pool", lambda e: e.memset(xst.t[:, :], 0.0), writes=[xst.r])
                        k.dma("sp", xst.t[0:4, :], xs, xst.r, writes=[xst.r])
                    else:
                        t0n = (T0 + 4 + q) * 128
                        k.dma("sp", xst.t[:, :], xp[t0n:t0n + 128, :], xst.r, writes=[xst.r])
                    pf_u = True
                uu = [SL.get(), SL.get()]
                for fc in range(8):
                    if False and q == 0 and ntile == 4:
                        sl_ = uu[fc // 4][1]
                        lc_ = fc % 4
                        pb = mmP.next()
                        for hf in range(2):
                            prs = [(sl_.t[:, kc, lc_ * 128:(lc_ + 1) * 128], actT.t[:, kc, hf * 256:(hf + 1) * 256]) for kc in range(8)]

                            def fh(e, prs=prs, hf=hf, pb=pb):
                                ins = None
                                for i_, (l_, r_) in enumerate(prs):
                                    ins = e.matmul(pb.t[:, hf * 256:(hf + 1) * 256], l_, r_, start=(i_ == 0 and hf == 0), stop=(i_ == 7),
                                                   skip_group_check=True)
                                return ins
                            k.op("pe", fh, reads=[actT.res[2 * hf], actT.res[2 * hf + 1], sl_.r], writes=[pb.r])
                    else:
                        pb = proj_feat(uu[fc // 4][1], fc % 4, N, ntile)
                    k.op("act", lambda e: e.activation(out=fT.t[:, fc, 0:N], in_=pb.t[:, 0:N], func=AF.Relu), reads=[pb.r], writes=[fT.res[fc]])
                    k.op(PL() if fc % 2 == 0 and fc < 6 else "dve", lambda e: e.tensor_tensor(out=fT.t[:, fc, 0:N], in0=fT.t[:, fc, 0:N], in1=fT.t[:, fc, 0:N], op=ALU.mult),
                         reads=[fT.res[fc]], writes=[fT.res[fc]])
                for u_, _ in uu:
                    SL.rel(u_)
                if pf_u is not None:
                    pf_u = norm_phase(xst.t[:, :], xst.r)
                for nh in range(2):
                    u_, sl = SL.get()
                    for j in range(ntile):
                        pb = mmP.next()
                        pairs = [(fT.t[:, fc, j * 128:(j + 1) * 128], sl.t[:, fc, :]) for fc in range(8)]
                        if nh == 0 and j == 0:
                            for fc in range(8):
                                k.op("pe", lambda e: e.matmul(pb.t[:, :], pairs[fc][0], pairs[fc][1], start=(fc == 0), stop=(fc == 7)),
                                     reads=[fT.res[fc], sl.r], writes=[pb.r])
                        else:
                            mm_acc(pb.t[:, :], pairs, fT.res + [sl.r], pb.r)
                        xa = xres.t[:, j, nh * 512:(nh + 1) * 512]
                        k.op("dve", lambda e: e.tensor_tensor(out=xa, in0=xa, in1=pb.t[:, :], op=ALU.add), reads=[pb.r, xres.res[j]], writes=[xres.res[j]])
                    SL.rel(u_)
                if pf_u is not None:
                    k.tag = "pf"
                    xpose_phase(pf_u, q, g1b, actTs[(blk + 1) % 2])
                    k.tag = "E"
            for j in range(ntile):
                norm_tile(xres.t[:, j, :], xres.res[j], gfb, xres.t[:, j, :], xres.res[j])
                if sample:
                    k.dma("sp", ys, xres.t[0:4, j, :], xres.res[j], reads=[xres.res[j]], store=True)
                else:
                    t0 = (T0 + j) * 128
                    k.dma("sp", yp[t0:t0 + 128, :], xres.t[:, j, :], xres.res[j], reads=[xres.res[j]], store=True)

        def finish_o(hb, pbk):
            s = ss.next()
            for hh in range(2):
                k.op("act", lambda e: e.activation(out=junk.t[:, 0:256], in_=pbk.t[:, hh * 256:(hh + 1) * 256], func=AF.Square,
                                                   scale=1.0 / 16.0, accum_out=s.t[:, hh:hh + 1]), reads=[pbk.r], writes=[junk.r, s.r])
            k.op("act", lambda e: e.activation(out=s.t[:, 4:6], in_=s.t[:, 0:2], func=AF.Ln, bias=EPS), reads=[s.r], writes=[s.r])
            k.op("act", lambda e: e.activation(out=s.t[:, 2:4], in_=s.t[:, 4:6], func=AF.Exp, scale=-0.5), reads=[s.r], writes=[s.r])
            for hh in range(2):
                h = hb * 2 + hh
                k.op("dve", lambda e: e.scalar_tensor_tensor(out=obb.t[:, h * 256:(h + 1) * 256], in0=pbk.t[:, hh * 256:(hh + 1) * 256],
                                                             scalar=s.t[:, 2 + hh:3 + hh], in1=gsb.t[:, h * 256:(h + 1) * 256],
                                                             op0=ALU.mult, op1=ALU.mult), reads=[pbk.r, s.r, gsb.r], writes=[obb.r])

        def sample_attention():
            ab = accP.next()
            first = [True]

            def acc_mm(lhsT, rhs_buf, last):
                st = first[0]
                first[0] = False
                k.op("pe", lambda e: e.matmul(ab.t[:, 0:260], lhsT, rhs_buf.t[:].rearrange("p h e -> p (h e)"), start=st, stop=last),
                     reads=[rhs_buf.r, Eb.r], writes=[ab.r])
            for b in range(4):
                for g in range(3):
                    W, dil = GROUPS[g]
                    ck, cv = cache[g]
                    for (cin, cout, srcap, srcres) in ((ck, skv[g][0], sqk.t[b:b + 1, g, 256:512], sqk.res[g]),
                                                       (cv, skv[g][1], svv.t[b:b + 1, g, :], svv.res[g])):
                        rr = Res(f"cp{b}{g}{cout.name}")
                        k.dma("sp", cout[b, 0:W - 1, :], cin[b, 1:W, :], rr, store=True)
                        k.dma("sp", cout[b, W - 1:W, :], srcap, rr, reads=[srcres], store=True)
                    ks_ = ksel.next()
                    vs_ = vsel.next()
                    bcs = bcss.next()
                    vself = vselfs.next()
                    k.dma("sp", ks_.t[:, :], ck[b, 0:W:dil, :], ks_.r, writes=[ks_.r])
                    k.dma("sp", vs_.t[:, :, 0:64], cv[b, 0:W:dil, :].rearrange("p (h d) -> p h d", d=64), vs_.r, writes=[vs_.r])
                    pq = mmP.next()
                    k.op("pe", lambda e: e.matmul(pq.t[:, :], Eb.t[:, b, :], sqk.t[:, g, :], start=True, stop=True),
                         reads=[sqk.res[g], Eb.r], writes=[pq.r])
                    k.op("act", lambda e: e.activation(out=bcs.t[:, :], in_=pq.t[:, :], func=AF.Copy), reads=[pq.r], writes=[bcs.r])
                    pvv = mmP.next()
                    k.op("pe", lambda e: e.matmul(pvv.t[:, 0:256], Eb.t[:, b, :], svv.t[:, g, :], start=True, stop=True),
                         reads=[svv.res[g], Eb.r], writes=[pvv.r])
                    k.op("act", lambda e: e.activation(out=vself.t[:, :, 0:64], in_=pvv.t[:, 0:256].rearrange("p (h d) -> p h d", d=64), func=AF.Copy),
                         reads=[pvv.r], writes=[vself.r])
                    for which in range(2):
                        pr_ = prod.next()
                        kin = ks_.t[:, :] if which == 0 else bcs.t[:, 256:512]
                        kres = ks_.r if which == 0 else bcs.r
                        k.op("dve", lambda e: e.tensor_tensor(out=pr_.t[:, :], in0=kin, in1=bcs.t[:, 0:256], op=ALU.mult),
                             reads=[kres, bcs.r], writes=[pr_.r])
                        s4 = sc4.next()
                        k.op("dve", lambda e: e.reduce_sum(out=s4.t[:, :], in_=pr_.t[:, :].rearrange("p (h d) -> p h d", d=64), axis=AX.X),
                             reads=[pr_.r], writes=[s4.r])
                        k.op("act", lambda e: e.activation(out=s4.t[:, :], in_=s4.t[:, :], func=AF.Exp, scale=0.125), reads=[s4.r], writes=[s4.r])
                        pb_ = pvb.next()
                        vv = vs_ if which == 0 else vself
                        k.op("dve", lambda e: e.tensor_tensor(out=pb_.t[:], in0=vv.t[:], in1=s4.t[:, :].unsqueeze(2).to_broadcast([128, 4, 65]), op=ALU.mult),
                             reads=[vv.r, s4.r], writes=[pb_.r])
                        last = (b == 3 and g == 2 and which == 1)
                        acc_mm(Eb.t[:, (4 if which == 0 else 8) + b, :], pb_, last)
            a3 = ab.t[:, 0:260].rearrange("p (h e) -> p h e", e=65)
            rc = rec.next()
            k.op("dve", lambda e: e.tensor_scalar(out=rc.t[:, :].unsqueeze(2), in0=a3[:, :, 64:65], scalar1=1e-30, scalar2=None, op0=ALU.max),
                 reads=[ab.r], writes=[rc.r])
            k.op("dve", lambda e: e.reciprocal(out=rc.t[:, :], in_=rc.t[:, :]), reads=[rc.r], writes=[rc.r])
            ob_ = oab.next()
            k.op("dve", lambda e: e.tensor_tensor(out=ob_.t[:, :].rearrange("p (h d) -> p h d", d=64), in0=a3[:, :, 0:64],
                                                  in1=rc.t[:, :].unsqueeze(2).to_broadcast([128, 4, 64]), op=ALU.mult),
                 reads=[ab.r, rc.r], writes=[ob_.r])

            def dst(pv, pb2, c0):
                cp3("act", oaT.t[:, :, 0:128], pv[:, 0:256].rearrange("p (c t) -> p c t", c=2), pb2, oaT.res[0])
            transposes(ob_, lambda c: ob_.t[:, c * 128:(c + 1) * 128], 2, dst, None)

        def gla_sample():
            po = [accP.next(), accP.next()]
            for b in range(4):
                pvs = []
                for half in range(2):
                    pv_ = mmP.next()
                    k.op("pe", lambda e: e.matmul(pv_.t[:, :], Eb.t[:, b, :], vbf.t[:, half * 512:(half + 1) * 512], start=True, stop=True),
                         reads=[vbf.r, Eb.r], writes=[pv_.r])
                    pvs.append(pv_)
                for h in range(4):
                    si = sin_.next()
                    k.dma("sp", si.t[:, :], sg[b, h], si.r, writes=[si.r])
                    k.op("dve", lambda e: e.tensor_scalar(out=si.t[:, :], in0=si.t[:, :], scalar1=e1.t[:, h, b:b + 1], scalar2=None, op0=ALU.mult),
                         reads=[si.r, e1.r], writes=[si.r])
                    sn = snew.next()
                    pv_ = pvs[h // 2]
                    k.op("dve", lambda e: e.scalar_tensor_tensor(out=sn.t[:, :], in0=pv_.t[:, (h % 2) * 256:(h % 2 + 1) * 256],
                                                                 scalar=kbT.t[:, h, b:b + 1], in1=si.t[:, :], op0=ALU.mult, op1=ALU.add),
                         reads=[pv_.r, kbT.res[h], si.r], writes=[sn.r])
                    k.dma("sp", sgla[b, h], sn.t[:, :], sn.r, reads=[sn.r], store=True)
                    qs = qsel.next()
                    k.op("dve", lambda e: e.tensor_scalar(out=qs.t[:, :], in0=Eb.t[:, 4 + b, :], scalar1=qbT.t[:, h, b:b + 1], scalar2=None, op0=ALU.mult),
                         reads=[Eb.r, qbT.res[h]], writes=[qs.r])
                    pk_ = po[h // 2]
                    k.op("pe", lambda e: e.matmul(pk_.t[:, (h % 2) * 256:(h % 2 + 1) * 256], qs.t[:, :], sn.t[:, :], start=(b == 0 and h % 2 == 0), stop=(b == 3),
                                                  skip_group_check=True),
                         reads=[qs.r, sn.r], writes=[pk_.r])
            for hb in range(2):
                finish_o(hb, po[hb])
            return po

        for blk in range(DBG["nprompt"]):
            k.epoch = 1 + blk // 4
            run_block(blk)
        k.barrier()
        es_p.close()
        Eb = Buf(k, "Eb", [128, 12, 128], F32)
        k.dma("sp", Eb.t[:], c_E, Eb.r, writes=[Eb.r])
        sqk = Buf(k, "sqk", [128, 3, 512], F32, nres=3)
        svv = Buf(k, "svv", [128, 3, 256], F32, nres=3)
        vbf = Buf(k, "vbf", [128, D], F32)
        ksel = Rot([Buf(k, f"ksel{i}", [128, 256], F32) for i in range(3)])
        vsel = Rot([Buf(k, f"vsel{i}", [128, 4, 65], F32) for i in range(3)])
        vselfs = Rot([Buf(k, f"vself{i}", [128, 4, 65], F32) for i in range(2)])
        bcss = Rot([Buf(k, f"bcs{i}", [128, 512], F32) for i in range(2)])
        prod = Rot([Buf(k, f"prod{i}", [128, 256], F32) for i in range(4)])
        sc4 = Rot([Buf(k, f"sc4{i}", [128, 4], F32) for i in range(4)])
        pvb = Rot([Buf(k, f"pvb{i}", [128, 4, 65], F32) for i in range(4)])
        sin_ = Rot([Buf(k, f"sin{i}", [128, 256], F32) for i in range(3)])
        snew = Rot([Buf(k, f"snew{i}", [128, 256], F32) for i in range(3)])
        qsel = Rot([Buf(k, f"qsel{i}", [128, 128], F32) for i in range(3)])
        k.epoch = 3
        for vs in vsel.items + vselfs.items:
            k.op("pool", lambda e, vs=vs: e.memset(vs.t[:], 1.0), writes=[vs.r])
        if DBG["sample"]:
            run_block(NBLK - 1)
        k.finish()
    return nc


_NC = None


def _consts():
    half = 32
    inv = (10000.0 ** (-np.arange(half, dtype=np.float32) / half)).astype(np.float32)
    pos = np.zeros((128, 33), np.float32)
    for T in range(32):
        pos[:, T] = T * 128 + np.arange(128)
    pos[:, 32] = PAST
    ang = pos[:, :, None] * inv[None, None, :]
    c_cos = np.cos(ang).astype(np.float32)
    c_sin = np.sin(ang).astype(np.float32)
    p = np.arange(128)[:, None]
    f = np.arange(128)[None, :]
    m = np.zeros((128, 9, 128), np.float32)
    for g, (W, d) in enumerate(GROUPS):
        mod = ((f - p) % d) == 0
        m[:, 3 * g + 0, :] = mod & (f >= p)
        m[:, 3 * g + 1, :] = mod
        m[:, 3 * g + 2, :] = mod & (f <= p)
    tri = (p <= f).astype(np.float32)
    c_tri = np.zeros((128, 2, 128), np.float32)
    c_tri[:, 0, :] = tri / 16.0
    c_tri[:, 1, :] = np.eye(128, dtype=np.float32) / 16.0
    E = np.zeros((128, 12, 128), np.float32)
    for b in range(4):
        E[b, b, :] = 1.0
        E[:, 4 + b, b] = 1.0
        E[0, 8 + b, b] = 1.0
    return dict(c_cos=c_cos, c_sin=c_sin, c_mask=m, c_id=np.eye(128, dtype=np.float32), c_tri=c_tri, c_tri1=tri, c_E=E)


def kernel(x_prompt, x_sample, cache_a1_k, cache_a1_v, cache_a2_k, cache_a2_v, cache_a3_k, cache_a3_v,
           state_gla, g_norm1, w_in, w_gk2, b_gk, g_gla, w_pa, w_pb, w_o, g_norm2, w_up, w_down, g_final):
    global _NC
    f = lambda a: np.ascontiguousarray(np.asarray(a, dtype=np.float32))
    if _NC is None:
        _NC = build()
    nc = _NC
    cs = _consts()
    caches = [(f(cache_a1_k)[0], f(cache_a1_v)[0]), (f(cache_a2_k)[0], f(cache_a2_v)[0]), (f(cache_a3_k)[0], f(cache_a3_v)[0])]
    xp_, xs_ = f(x_prompt), f(x_sample)
    sg_ = f(state_gla)[0]
    rep = lambda v: np.ascontiguousarray(np.broadcast_to(f(v).reshape(1, -1), (128, f(v).size)))
    gT = lambda v: np.ascontiguousarray(f(v).reshape(8, 128).T)
    shared = dict(g1=gT(g_norm1), g2=gT(g_norm2), gf=rep(g_final), gg=rep(g_gla), w_in=f(w_in)[0],
                  wgk=np.ascontiguousarray(np.concatenate([f(w_gk2)[0], f(b_gk)[0][None, :]], axis=0)),
                  w_pa=f(w_pa)[0], w_pb=f(w_pb)[0], w_o=f(w_o)[0], w_up=f(w_up)[0], w_dn=f(w_down)[0], **cs)
    in_maps = []
    for c in range(8):
        m = dict(shared)
        m["xp"] = xp_[c]
        m["xs"] = np.ascontiguousarray(xs_[4 * c:4 * c + 4, 0, :])
        for g in range(3):
            W = GROUPS[g][0]
            m[f"c{g}k"] = np.ascontiguousarray(caches[g][0][4 * c:4 * c + 4].reshape(4, W, 256))
            m[f"c{g}v"] = np.ascontiguousarray(caches[g][1][4 * c:4 * c + 4].reshape(4, W, 256))
        m["sg"] = np.ascontiguousarray(sg_[4 * c:4 * c + 4])
        in_maps.append(m)
    ncr = DBG["ncores"]
    res = run_bass_kernel_spmd(nc, in_maps[:ncr], core_ids=list(range(ncr)))
    R = list(res.results) + [res.results[0]] * (8 - ncr)
    y_prompt = np.stack([R[c]["yp"] for c in range(8)], axis=0)
    y_sample = np.concatenate([R[c]["ys"] for c in range(8)], axis=0).reshape(32, 1, D)
    outs = [y_prompt, y_sample]
    for g in range(3):
        W = GROUPS[g][0]
        for kv in ("k", "v"):
            outs.append(np.stack([R[c][f"p{g}{kv}"] for c in range(8)], axis=0).reshape(1, 8, W, 4, 64))
    outs.append(np.stack([R[c]["pgla"] for c in range(8)], axis=0).reshape(1, 8, 4, 128, 256))
    for g in range(3):
        W = GROUPS[g][0]
        for kv in ("k", "v"):
            outs.append(np.concatenate([R[c][f"s{g}{kv}"] for c in range(8)], axis=0).reshape(1, 32, W, 4, 64))
    outs.append(np.concatenate([R[c]["sgla"] for c in range(8)], axis=0).reshape(1, 32, 4, 128, 256))
    return tuple(np.ascontiguousarray(o.astype(np.float32)) for o in outs)
```
